# Optimizing a Trainium2 kernel written in Bass

```python
import math
import jax, jax.numpy as jnp
from jax import lax
import numpy as np

D_MODEL = 1024
BATCH = 32
SEQ = 2048
DEPTH = 4

N_META = 16
ATTN_HEADS = 4
ATTN_DK = 64
ATTN_DV = 2 * ATTN_DK
ATTN_WIDTH = ATTN_HEADS * ATTN_DV
CONV_WIDTH = D_MODEL - ATTN_WIDTH
CONV_K = 3
MIX_WIDTH = ATTN_WIDTH + CONV_WIDTH
Q_COLS = ATTN_HEADS * 2 * ATTN_DK
K_COLS = Q_COLS
V_COLS = ATTN_WIDTH
IN_COLS = Q_COLS + K_COLS + V_COLS + 3 * CONV_WIDTH
Q_BLOCK = 128
REL_BUCKETS = 32
REL_MAX_DIST = 128
PEER_HEADS = 8
PEER_NKEYS = 128
PEER_N = PEER_NKEYS * PEER_NKEYS
PEER_DQ = 256
PEER_TOPK = 16
PEER_CHUNK = 256
DEEPNORM_ALPHA = (2 * DEPTH) ** 0.25
DEEPNORM_BETA = (8 * DEPTH) ** -0.25
LN_EPS = 1e-5
RMS_EPS = 1e-5
NEG_BIG = -1e30

kernel_name = "hymba_diffattn_shortconv_peer_deepnorm"


def layer_norm(x, g, b):
    xf = x.astype(jnp.float32)
    mu = xf.mean(-1, keepdims=True)
    var = jnp.square(xf - mu).mean(-1, keepdims=True)
    return ((xf - mu) * lax.rsqrt(var + LN_EPS) * g.astype(jnp.float32) + b.astype(jnp.float32)).astype(x.dtype)


def t5_bucket(qpos, kpos):
    n = jnp.maximum(qpos[:, None] - kpos[None, :], 0)
    max_exact = REL_BUCKETS // 2
    nf = jnp.maximum(n, 1).astype(jnp.float32)
    large = max_exact + (jnp.log(nf / max_exact) / math.log(REL_MAX_DIST / max_exact)
                         * (REL_BUCKETS - max_exact)).astype(jnp.int32)
    large = jnp.minimum(large, REL_BUCKETS - 1)
    return jnp.where(n < max_exact, n, large)


def diff_attn_block(qb, qpos, k, v, kpos, rel_bias, lam):
    logits = jnp.einsum('bqhmd,bkhmd->bhmqk', qb.astype(jnp.float32), k.astype(jnp.float32)) * (ATTN_DK ** -0.5)
    bias = rel_bias.astype(jnp.float32)[t5_bucket(qpos, kpos)]
    bias = jnp.transpose(bias, (2, 0, 1))[None, :, None]
    mask = kpos[None, :] <= qpos[:, None]
    logits = jnp.where(mask, logits + bias, NEG_BIG)
    p = jax.nn.softmax(logits, axis=-1)
    attn = p[:, :, 0] - lam * p[:, :, 1]
    return jnp.einsum('bhqk,bkhd->bqhd', attn, v.astype(jnp.float32))


def differential_attention(q, k, v, rel_bias, lam, subln_g, lam_init):
    Bn, L = q.shape[0], q.shape[1]
    kpos = jnp.arange(L, dtype=jnp.int32)
    mpos = jnp.arange(N_META, dtype=jnp.int32)
    out_meta = diff_attn_block(q[:, :N_META], mpos, k[:, :N_META], v[:, :N_META], mpos, rel_bias, lam)
    n_blk = (L - N_META) // Q_BLOCK
    q_real = q[:, N_META:].reshape(Bn, n_blk, Q_BLOCK, ATTN_HEADS, 2, ATTN_DK)
    q_real = jnp.moveaxis(q_real, 1, 0)

    def one_block(args):
        qb, bi = args
        qpos = N_META + bi * Q_BLOCK + jnp.arange(Q_BLOCK, dtype=jnp.int32)
        return diff_attn_block(qb, qpos, k, v, kpos, rel_bias, lam)

    out_real = lax.map(one_block, (q_real, jnp.arange(n_blk, dtype=jnp.int32)))
    out_real = jnp.moveaxis(out_real, 0, 1).reshape(Bn, L - N_META, ATTN_HEADS, ATTN_DV)
    out = jnp.concatenate([out_meta, out_real], axis=1)
    out = out * lax.rsqrt(jnp.square(out).mean(-1, keepdims=True) + RMS_EPS)
    out = out * subln_g.astype(jnp.float32) * (1.0 - lam_init)
    return out.reshape(Bn, L, ATTN_WIDTH).astype(q.dtype)


def short_conv(z, w):
    return lax.conv_general_dilated(z, w[:, None, :].astype(z.dtype), window_strides=(1,),
                                    padding=[(CONV_K - 1, 0)],
                                    dimension_numbers=('NWC', 'WIO', 'NWC'),
                                    feature_group_count=CONV_WIDTH)


def peer(x, w_q, sub_keys, u, v):
    Bn, L, _ = x.shape
    T = Bn * L
    n_chunks = -(-T // PEER_CHUNK)
    pad = n_chunks * PEER_CHUNK - T
    xt = jnp.pad(x.reshape(T, D_MODEL), ((0, pad), (0, 0))).reshape(n_chunks, PEER_CHUNK, D_MODEL)

    def one_chunk(xc):
        q = (xc @ w_q).reshape(PEER_CHUNK, PEER_HEADS, 2, PEER_DQ // 2)
        s = jnp.einsum('thpd,hpnd->thpn', q.astype(jnp.float32), sub_keys.astype(jnp.float32))
        s1, i1 = lax.top_k(s[:, :, 0], PEER_TOPK)
        s2, i2 = lax.top_k(s[:, :, 1], PEER_TOPK)
        cand = (s1[..., :, None] + s2[..., None, :]).reshape(PEER_CHUNK, PEER_HEADS, PEER_TOPK * PEER_TOPK)
        sc, ci = lax.top_k(cand, PEER_TOPK)
        idx = (jnp.take_along_axis(i1, ci // PEER_TOPK, axis=-1) * PEER_NKEYS
               + jnp.take_along_axis(i2, ci % PEER_TOPK, axis=-1))
        g = jax.nn.softmax(sc, axis=-1)
        ue = u[idx]
        ve = v[idx]
        act = jax.nn.gelu(jnp.einsum('td,thkd->thk', xc, ue), approximate=False)
        return jnp.einsum('thk,thkd->td', (g * act).astype(xc.dtype), ve)

    out = lax.map(one_chunk, xt)
    return out.reshape(n_chunks * PEER_CHUNK, D_MODEL)[:T].reshape(Bn, L, D_MODEL)


def setup_inputs(seed: int = 0) -> dict:
    key = jax.random.key(seed)
    ks = jax.random.split(key, 24)
    f32 = jnp.float32
    nrm = lambda k, shape, s: jax.random.normal(k, shape, f32) * s
    col_scale = jnp.concatenate([
        jnp.ones((Q_COLS + K_COLS,), f32), jnp.full((V_COLS,), DEEPNORM_BETA, f32),
        jnp.ones((2 * CONV_WIDTH,), f32), jnp.full((CONV_WIDTH,), DEEPNORM_BETA, f32)])
    return {
        "x": nrm(ks[0], (BATCH, SEQ, D_MODEL), 1.0),
        "meta_tokens": nrm(ks[1], (N_META, D_MODEL), 1.0),
        "ln_in_g": 1.0 + nrm(ks[2], (D_MODEL,), 0.02),
        "ln_in_b": nrm(ks[3], (D_MODEL,), 0.02),
        "rel_bias": nrm(ks[4], (REL_BUCKETS, ATTN_HEADS), 0.5),
        "w_in": nrm(ks[5], (DEPTH, D_MODEL, IN_COLS), D_MODEL ** -0.5) * col_scale,
        "conv_w": nrm(ks[6], (DEPTH, CONV_K, CONV_WIDTH), CONV_K ** -0.5),
        "lambda_q1": nrm(ks[7], (DEPTH, ATTN_DK), 0.1),
        "lambda_k1": nrm(ks[8], (DEPTH, ATTN_DK), 0.1),
        "lambda_q2": nrm(ks[9], (DEPTH, ATTN_DK), 0.1),
        "lambda_k2": nrm(ks[10], (DEPTH, ATTN_DK), 0.1),
        "subln_g": 1.0 + nrm(ks[11], (DEPTH, ATTN_DV), 0.02),
        "w_out": nrm(ks[12], (DEPTH, MIX_WIDTH, D_MODEL), MIX_WIDTH ** -0.5) * DEEPNORM_BETA,
        "ln1_g": 1.0 + nrm(ks[13], (DEPTH, D_MODEL), 0.02),
        "ln1_b": nrm(ks[14], (DEPTH, D_MODEL), 0.02),
        "peer_w_q": nrm(ks[15], (DEPTH, D_MODEL, PEER_HEADS * PEER_DQ), D_MODEL ** -0.5),
        "peer_sub_keys": nrm(ks[16], (DEPTH, PEER_HEADS, 2, PEER_NKEYS, PEER_DQ // 2), (PEER_DQ // 2) ** -0.5),
        "peer_u": nrm(ks[17], (DEPTH, PEER_N, D_MODEL), D_MODEL ** -0.5),
        "peer_v": nrm(ks[18], (DEPTH, PEER_N, D_MODEL), DEEPNORM_BETA * PEER_HEADS ** -0.5),
        "ln2_g": 1.0 + nrm(ks[19], (DEPTH, D_MODEL), 0.02),
        "ln2_b": nrm(ks[20], (DEPTH, D_MODEL), 0.02),
    }


def reference(x, meta_tokens, ln_in_g, ln_in_b, rel_bias, w_in, conv_w, lambda_q1, lambda_k1,
              lambda_q2, lambda_k2, subln_g, w_out, ln1_g, ln1_b, peer_w_q, peer_sub_keys,
              peer_u, peer_v, ln2_g, ln2_b):
    Bn = x.shape[0]
    meta = jnp.broadcast_to(meta_tokens[None].astype(x.dtype), (Bn, N_META, D_MODEL))
    h = jnp.concatenate([meta, x], axis=1)
    h = layer_norm(h, ln_in_g, ln_in_b)
    L = h.shape[1]
    splits = [Q_COLS, Q_COLS + K_COLS, Q_COLS + K_COLS + V_COLS,
              Q_COLS + K_COLS + V_COLS + CONV_WIDTH, Q_COLS + K_COLS + V_COLS + 2 * CONV_WIDTH]
    for l in range(DEPTH):
        lam_init = 0.8 - 0.6 * math.exp(-0.3 * l)
        lam = (jnp.exp(jnp.dot(lambda_q1[l].astype(jnp.float32), lambda_k1[l].astype(jnp.float32)))
               - jnp.exp(jnp.dot(lambda_q2[l].astype(jnp.float32), lambda_k2[l].astype(jnp.float32)))
               + lam_init)
        proj = h @ w_in[l]
        q, k, v, gb, gc, z = jnp.split(proj, splits, axis=-1)
        q = q.reshape(Bn, L, ATTN_HEADS, 2, ATTN_DK)
        k = k.reshape(Bn, L, ATTN_HEADS, 2, ATTN_DK)
        v = v.reshape(Bn, L, ATTN_HEADS, ATTN_DV)
        attn_out = differential_attention(q, k, v, rel_bias, lam, subln_g[l], lam_init)
        conv_out = gb * short_conv(gc * z, conv_w[l])
        mix = jnp.concatenate([attn_out, conv_out], axis=-1) @ w_out[l]
        h = layer_norm(DEEPNORM_ALPHA * h + mix, ln1_g[l], ln1_b[l])
        ffn = peer(h, peer_w_q[l], peer_sub_keys[l], peer_u[l], peer_v[l])
        h = layer_norm(DEEPNORM_ALPHA * h + ffn, ln2_g[l], ln2_b[l])
    return h[:, N_META:]
```

```python
import contextlib
import math
import numpy as np
import ml_dtypes
import concourse.bass as bass
import concourse.mybir as mybir
from concourse.bass_utils import run_bass_kernel_spmd

F32 = mybir.dt.float32
BF16 = mybir.dt.bfloat16
I32 = mybir.dt.int32
U32 = mybir.dt.uint32
AF = mybir.ActivationFunctionType
ALU = mybir.AluOpType
AX = mybir.AxisListType

ENGS = ("sync", "scalar", "vector", "gpsimd", "tensor")

D = 1024
SEQ = 2048
NMETA = 16
L = SEQ + NMETA
NT = 17
LP = NT * 128
DEPTH = 4
NSEQ = 4
ALPHA = (2 * DEPTH) ** 0.25
LN_EPS = 1e-5
RMS_EPS = 1e-5
NE = 16384
QC = 3
RING = 8
NB = 4
LAYERS_PER_LAUNCH = 4


ALL_BUFS = []


class Buf:
    __slots__ = ("w", "r")

    def __init__(self):
        self.w = None
        self.r = {}
        ALL_BUFS.append(self)


class DSem:
    def __init__(self, sem):
        self.sem = sem
        self.cnt = 0


class Prog:
    def __init__(self, nc):
        self.nc = nc
        self.stack = contextlib.ExitStack()
        self.ops = {e: [] for e in ENGS}
        self.cnt = {e: 0 for e in ENGS}
        self.seen = {e: {} for e in ENGS}
        self.sem = {e: self.stack.enter_context(nc.semaphore("p_" + e)) for e in ENGS}
        self.nsem = 0
        self.nereg = None
        self.pre = None
        self.dsems = []
        self.loopvar = {}
        self.B1 = self.stack.enter_context(nc.semaphore("B1"))
        self.B2 = self.stack.enter_context(nc.semaphore("B2"))

    def sbuf(self, name, shape, dt):
        return self.stack.enter_context(self.nc.sbuf_tensor("sb_" + name, shape, dt))

    def psum(self, name, shape, dt):
        return self.stack.enter_context(self.nc.psum_tensor("ps_" + name, shape, dt))

    def dsem(self):
        self.nsem += 1
        d = DSem(self.stack.enter_context(self.nc.semaphore(f"d{self.nsem}")))
        self.dsems.append(d)
        return d

    def _wait(self, eng, tok):
        if tok is None:
            return
        sem, val = tok
        k = id(sem)
        if self.seen[eng].get(k, 0) >= val:
            return
        self.seen[eng][k] = val
        self.ops[eng].append(lambda e, sem=sem, val=val: e.wait_ge(sem, val))

    def _deps(self, eng, reads, writes):
        for b in reads:
            self._wait(eng, b.w)
        for b in writes:
            self._wait(eng, b.w)
            for t in b.r.values():
                self._wait(eng, t)

    def _commit(self, tok, reads, writes):
        k = id(tok[0])
        for b in reads:
            o = b.r.get(k)
            if o is None or o[1] < tok[1]:
                b.r[k] = tok
        for b in writes:
            b.w = tok
            b.r = {}

    def op(self, eng, reads, writes, name, **kw):
        return self.opm(eng, reads, writes, [(name, kw)])

    def opm(self, eng, reads, writes, insts):
        self._deps(eng, reads, writes)
        self.cnt[eng] += 1
        sem = self.sem[eng]
        insts = list(insts)

        def run(e, insts=insts, sem=sem):
            r = None
            for name, kw in insts:
                try:
                    r = getattr(e, name)(**kw)
                except Exception:
                    print("FAILED OP", name, {k: (getattr(v, "shape", v), getattr(v, "ap", None)) for k, v in kw.items()})
                    raise
            r.then_inc(sem, 1)
        self.ops[eng].append(run)
        tok = (sem, self.cnt[eng])
        self._commit(tok, reads, writes)
        return tok

    def dma(self, eng, ds, reads, writes, insts):
        self._deps(eng, reads, writes)
        sem = ds.sem
        insts = list(insts)

        def run(e, insts=insts, sem=sem, eng=eng):
            for name, kw in insts:
                if any(callable(v) for v in kw.values()):
                    kw = {k: (v(self.loopvar[eng]) if callable(v) else v) for k, v in kw.items()}
                if kw.get("bounds_check") == "NEREG":
                    if self.nereg is None:
                        self.nereg = e.to_reg(NE - 1)
                    kw = dict(kw)
                    kw["bounds_check"] = self.nereg
                try:
                    getattr(e, name)(**kw).then_inc(sem, 16)
                except Exception:
                    print("FAILED DMA", name, {k: (getattr(v, "shape", v), getattr(v, "ap", None)) for k, v in kw.items()})
                    raise
        self.ops[eng].append(run)
        ds.cnt += 16 * len(insts)
        tok = (sem, ds.cnt)
        self._commit(tok, reads, writes)
        return tok

    def barrier(self):
        toks = [(self.sem[e], self.cnt[e]) for e in ENGS if self.cnt[e] > 0]
        for e in ENGS:
            for t in toks:
                self._wait(e, t)

    def iter_sync(self, it):
        toks = [(self.sem[e], self.cnt[e]) for e in ENGS if self.cnt[e] > 0]
        toks += [(d.sem, d.cnt) for d in self.dsems if d.cnt > 0]
        allsems = [self.sem[e] for e in ENGS] + [d.sem for d in self.dsems]
        B1, B2 = self.B1, self.B2
        for en in ENGS:
            for t in toks:
                self._wait(en, t)
            if en == "sync":
                def run(e, it=it, allsems=allsems):
                    n = it if it is not None else (self.loopvar["sync"] + 2)
                    e.sem_inc(B1, 1)
                    e.wait_ge(B1, n * len(ENGS))
                    for sm in allsems:
                        e.sem_clear(sm)
                    e.drain().then_inc(B2, 1)
            else:
                def run(e, it=it, en=en):
                    n = it if it is not None else (self.loopvar[en] + 2)
                    if en == "gpsimd":
                        e.dma_reset()
                    e.sem_inc(B1, 1)
                    e.wait_ge(B2, n)
            self.ops[en].append(run)
        self.cnt = {e: 0 for e in ENGS}
        self.seen = {e: {} for e in ENGS}
        for d in self.dsems:
            d.cnt = 0
        for b in ALL_BUFS:
            b.w = None
            b.r = {}

    def start_body(self):
        self.iter_sync(1)
        self.pre = self.ops
        self.ops = {e: [] for e in ENGS}

    def emit(self, niter):
        with self.nc.Block() as block:
            for en in ENGS:
                pre = self.pre[en]
                ops = self.ops[en]

                def body(e, pre=pre, ops=ops, en=en):
                    for f in pre:
                        f(e)
                    with e.Fori(0, niter) as s:
                        self.loopvar[en] = s
                        for f in ops:
                            f(e)
                getattr(block, en)(body)


def _bucket_thresholds():
    n = np.arange(0, 4096, dtype=np.int32)
    nf = np.maximum(n, 1).astype(np.float32)
    large = 16 + (np.log(nf / np.float32(16)) / np.float32(math.log(128 / 16)) * np.float32(16)).astype(np.int32)
    large = np.minimum(large, 31)
    bucket = np.where(n < 16, n, large)
    thr = []
    for b in range(1, 32):
        idx = np.nonzero(bucket >= b)[0]
        thr.append(int(idx[0]))
    return thr


def build(NL, do_ln_in):
    nc = bass.Bass("TRN2", target_bir_lowering=False)

    def din(name, shape, dt=F32):
        return nc.dram_tensor(name, shape, dt, kind="ExternalInput").ap()

    def dint(name, shape, dt):
        return nc.dram_tensor(name, shape, dt, kind="Internal").ap()

    h_in = din("h_in", [NSEQ, LP, D])
    h_out = nc.dram_tensor("h_out", [NSEQ, LP, D], F32, kind="ExternalOutput").ap()
    lnin_g = din("lnin_g", [D]); lnin_b = din("lnin_b", [D])
    rel_bias = din("rel_bias", [32, 4])
    laminit = din("laminit", [NL])
    identF_d = din("identF", [128, 128]); identB_d = din("identB", [128, 128], BF16)
    zmask_d = din("zmask", [128, 255]); dist_d = din("dist", [128, 256]); iota_d = din("iota16", [128, 16])
    W = []
    for l in range(NL):
        W.append(dict(
            w_in=din(f"w_in{l}", [D, 3072]), w_out=din(f"w_out{l}", [D, D]), w_q=din(f"w_q{l}", [D, 2048]),
            keysT=din(f"keysT{l}", [2048, 128]), u=din(f"u{l}", [NE, D]), v=din(f"v{l}", [NE, D]),
            convw=din(f"convw{l}", [512, 3]), lq1=din(f"lq1_{l}", [64]), lk1=din(f"lk1_{l}", [64]),
            lq2=din(f"lq2_{l}", [64]), lk2=din(f"lk2_{l}", [64]), subg=din(f"subg{l}", [128]),
            ln1g=din(f"ln1g{l}", [D]), ln1b=din(f"ln1b{l}", [D]), ln2g=din(f"ln2g{l}", [D]), ln2b=din(f"ln2b{l}", [D]),
            winb=dint(f"winb{l}", [D, 3072], BF16), woutb=dint(f"woutb{l}", [D, D], BF16),
            wqb=dint(f"wqb{l}", [D, 2048], BF16), keysb=dint(f"keysb{l}", [2048, 128], BF16),
            uvb=dint(f"uvb{l}", [NE, 2048], BF16)))

    P = Prog(nc)
    h = P.sbuf("h", [128, NT, D], F32)
    X = P.sbuf("X", [128, 8 * LP], BF16)
    hT = X[:, :].rearrange("p (c t) -> p c t", c=8)
    ring = X[:, 0:RING * 2048].rearrange("p (s n) -> p s n", s=RING)
    LNP = P.sbuf("LNP", [128, 2, D], F32)
    identF = P.sbuf("identF", [128, 128], F32)
    identB = P.sbuf("identB", [128, 128], BF16)
    zmask = P.sbuf("zmask", [128, 255], F32)
    dist = P.sbuf("dist", [128, 256], F32)
    iota16 = P.sbuf("iota16", [128, 16], F32)
    rb = P.sbuf("rb", [128, 128], F32)
    dl = P.sbuf("dl", [128, 124], F32)
    negc = P.sbuf("negc", [128, 4], F32)
    Mt = P.sbuf("Mt", [128, 4, 256], BF16)
    lamt = P.sbuf("lamt", [128, 4 * NL + 4], F32)
    neglam = P.sbuf("neglam", [128, NL], F32)
    gsc = P.sbuf("gsc", [128, NL], F32)
    lnst = P.sbuf("lnst", [128, NT, 12], F32)
    lnmv = P.sbuf("lnmv", [128, NT, 2], F32)
    lnve = P.sbuf("lnve", [128, NT], F32)
    lnrs = P.sbuf("lnrs", [128, NT], F32)
    YB = 74 * 1024
    Y = P.sbuf("Y", [128, YB // 2], BF16)
    ps = P.psum("ps", [128, 8, 512], F32)
    b_ps = [Buf() for _ in range(8)]

    class Arena:
        def __init__(self):
            self.off = 0

        def reset(self):
            self.off = 0

        def alloc(self, shape, dt):
            n = int(np.prod(shape))
            esz = 2 if dt == BF16 else 4
            nb = (n * esz + 63) // 64 * 64
            assert self.off + nb <= YB, (self.off, nb)
            v = Y[:, self.off // 2:(self.off + n * esz) // 2]
            self.off += nb
            if dt != BF16:
                v = v.bitcast(dt)
            if len(shape) == 1:
                return v
            if len(shape) == 2:
                return v.rearrange("p (a b) -> p a b", a=shape[0])
            if len(shape) == 3:
                return v.rearrange("p (a b c) -> p a b c", a=shape[0], b=shape[1])
            raise ValueError
    A = Arena()

    b_h = [Buf() for _ in range(NT)]
    b_hT = [Buf() for _ in range(NT)]
    b_const = Buf()
    b_lnp = Buf()
    b_ln = Buf()
    ld = P.dsem()
    s_lnp = P.dsem()

    P.dma("sync", ld, [], [b_const], [
        ("dma_start", dict(out=identF[:], in_=identF_d)), ("dma_start", dict(out=identB[:], in_=identB_d)),
        ("dma_start", dict(out=zmask[:], in_=zmask_d)), ("dma_start", dict(out=dist[:], in_=dist_d)),
        ("dma_start", dict(out=iota16[:], in_=iota_d)),
        ("dma_start", dict(out=rb[:], in_=rel_bias.rearrange("a b -> (a b)").partition_broadcast(128))),
        ("dma_start", dict(out=lamt[:, 4 * NL:4 * NL + NL], in_=laminit.partition_broadcast(128)))])

    A.reset()
    NSTG = 4
    stg_f = [A.alloc([1, 2048], F32) for _ in range(NSTG)]
    stg_b = [A.alloc([1, 2048], BF16) for _ in range(NSTG)]
    b_sf = [Buf() for _ in range(NSTG)]; b_sb = [Buf() for _ in range(NSTG)]
    s_in = [P.dsem() for _ in range(NSTG)]; s_out = [P.dsem() for _ in range(NSTG)]
    cvn = [0]
    b_wconv = [Buf() for _ in range(NL)]
    cv_jobs = []

    def conv_chunk(src_ap, dst_ap, shp, l):
        cv_jobs.append((src_ap, dst_ap, shp))

    def cv_views(i, shp):
        n = int(np.prod(shp))
        sf = stg_f[i][:, 0, 0:n]; sb = stg_b[i][:, 0, 0:n]
        if len(shp) == 2:
            return sf, sb, sf.rearrange("p (a c) -> p a c", a=shp[0]), sb.rearrange("p (a c) -> p a c", a=shp[0])
        return sf, sb, sf, sb

    def cv_load(n):
        src_ap, dst_ap, shp = cv_jobs[n]
        i = n % NSTG
        sf, sb, sfv, sbv = cv_views(i, shp)
        P.dma("sync", s_in[i], [], [b_sf[i]], [("dma_start", dict(out=sfv, in_=src_ap))])

    def cv_cast_store(n):
        src_ap, dst_ap, shp = cv_jobs[n]
        i = n % NSTG
        sf, sb, sfv, sbv = cv_views(i, shp)
        if n % 2:
            P.op("scalar", [b_sf[i]], [b_sb[i]], "activation", out=sb, in_=sf, func=AF.Copy)
        else:
            P.op("vector", [b_sf[i]], [b_sb[i]], "tensor_copy", out=sb, in_=sf)
        P.dma("sync", s_out[i], [b_sb[i]], [], [("dma_start", dict(out=dst_ap, in_=sbv))])

    def cv_run():
        nj = len(cv_jobs)
        for n in range(min(NSTG - 1, nj)):
            cv_load(n)
        for n in range(nj):
            if n + NSTG - 1 < nj:
                cv_load(n + NSTG - 1)
            cv_cast_store(n)

    def conv2d(src, dst, R, C, l):
        if C <= 1024:
            a = 2048 // C
            nchunk = R // (128 * a)
            sv = src.rearrange("(n p a) c -> n p a c", p=128, a=a)
            dv = dst.rearrange("(n p a) c -> n p a c", p=128, a=a)
            for n in range(nchunk):
                conv_chunk(sv[n], dv[n], (a, C), l)
        else:
            for r0 in range(0, R, 128):
                for c0 in range(0, C, 2048):
                    c1 = min(c0 + 2048, C)
                    conv_chunk(src[r0:r0 + 128, c0:c1], dst[r0:r0 + 128, c0:c1], (c1 - c0,), l)

    for l in range(NL):
        w = W[l]
        conv2d(w["w_in"], w["winb"], D, 3072, l)
        conv2d(w["w_out"], w["woutb"], D, D, l)
        conv2d(w["w_q"], w["wqb"], D, 2048, l)
        conv2d(w["keysT"], w["keysb"], 2048, 128, l)
        conv2d(w["u"], w["uvb"][:, 0:1024], NE, D, l)
        conv2d(w["v"], w["uvb"][:, 1024:2048], NE, D, l)
    cv_run()

    thr = _bucket_thresholds()
    acc_t = A.alloc([1, 256], F32)[:, 0, :]
    tmp_t = A.alloc([1, 256], F32)[:, 0, :]
    msk_t = A.alloc([1, 256], F32)[:, 0, :]
    lam_s = A.alloc([4, 64], F32)
    lam_j = A.alloc([1, 64], F32)[:, 0, :]
    b_t = Buf(); b_M = Buf(); b_lam = Buf()
    s_lam = P.dsem(); s_cw = P.dsem(); s_ky = P.dsem()
    P.op("vector", [b_const], [b_t], "tensor_tensor", out=dl[:], in0=rb[:, 4:128], in1=rb[:, 0:124], op=ALU.subtract)
    P.op("vector", [b_const], [b_t], "tensor_scalar", out=negc[:], in0=rb[:, 124:128], scalar1=-1.0, scalar2=None, op0=ALU.mult)
    P.op("vector", [b_const], [b_t], "tensor_scalar", out=msk_t, in0=dist[:], scalar1=0.0, scalar2=None, op0=ALU.is_ge)
    for hh in range(4):
        P.op("vector", [b_const], [b_t], "tensor_scalar", out=acc_t, in0=dist[:], scalar1=0.0, scalar2=rb[:, hh:hh + 1],
             op0=ALU.mult, op1=ALU.add)
        for b in range(1, 32):
            P.op("vector", [b_const, b_t], [b_t], "tensor_scalar", out=tmp_t, in0=dist[:], scalar1=float(thr[b - 1]),
                 scalar2=dl[:, (b - 1) * 4 + hh:(b - 1) * 4 + hh + 1], op0=ALU.is_ge, op1=ALU.mult)
            P.op("vector", [b_t], [b_t], "tensor_tensor", out=acc_t, in0=acc_t, in1=tmp_t, op=ALU.add)
        P.op("scalar", [b_t], [b_t], "activation", out=tmp_t, in_=acc_t, func=AF.Exp, bias=negc[:, hh:hh + 1], scale=1.0)
        P.op("vector", [b_t], [b_M, b_t], "tensor_tensor", out=Mt[:, hh, :], in0=tmp_t, in1=msk_t, op=ALU.mult)

    for l in range(NL):
        w = W[l]
        P.dma("sync", s_lam, [b_lam], [b_lam], [
            ("dma_start", dict(out=lam_s[:, 0, :], in_=w["lq1"].partition_broadcast(128))),
            ("dma_start", dict(out=lam_s[:, 1, :], in_=w["lk1"].partition_broadcast(128))),
            ("dma_start", dict(out=lam_s[:, 2, :], in_=w["lq2"].partition_broadcast(128))),
            ("dma_start", dict(out=lam_s[:, 3, :], in_=w["lk2"].partition_broadcast(128))),
            ("dma_start", dict(out=gsc[:, l:l + 1], in_=w["subg"].rearrange("(p o) -> p o", o=1)))])
        c0 = 4 * l
        P.op("vector", [b_lam], [b_lam], "scalar_tensor_tensor", out=lam_j, in0=lam_s[:, 0, :], scalar=1.0, in1=lam_s[:, 1, :],
             op0=ALU.mult, op1=ALU.mult, accum_out=lamt[:, c0:c0 + 1])
        P.op("vector", [b_lam], [b_lam], "scalar_tensor_tensor", out=lam_j, in0=lam_s[:, 2, :], scalar=1.0, in1=lam_s[:, 3, :],
             op0=ALU.mult, op1=ALU.mult, accum_out=lamt[:, c0 + 1:c0 + 2])
        P.op("scalar", [b_lam], [b_lam], "activation", out=lamt[:, c0 + 2:c0 + 4], in_=lamt[:, c0:c0 + 2], func=AF.Exp)
        P.op("vector", [b_lam, b_const], [b_lam], "tensor_tensor", out=lamt[:, c0:c0 + 1], in0=lamt[:, c0 + 3:c0 + 4],
             in1=lamt[:, c0 + 2:c0 + 3], op=ALU.subtract)
        P.op("vector", [b_lam, b_const], [b_lam], "tensor_tensor", out=neglam[:, l:l + 1], in0=lamt[:, c0:c0 + 1],
             in1=lamt[:, 4 * NL + l:4 * NL + l + 1], op=ALU.subtract)
        P.op("vector", [b_lam, b_const], [b_lam], "tensor_scalar", out=lamt[:, c0 + 1:c0 + 2],
             in0=lamt[:, 4 * NL + l:4 * NL + l + 1], scalar1=-1.0, scalar2=1.0, op0=ALU.mult, op1=ALU.add)
        P.op("vector", [b_lam], [b_lam], "tensor_tensor", out=gsc[:, l:l + 1], in0=gsc[:, l:l + 1], in1=lamt[:, c0 + 1:c0 + 2],
             op=ALU.mult)
    for e_ in ENGS:
        for i_ in range(NSTG):
            P._wait(e_, (s_out[i_].sem, s_out[i_].cnt))
    P.barrier()

    def layernorm(g_ap, b_ap):
        gi, bi = 0, 1
        P.dma("sync", s_lnp, [b_lnp], [b_lnp], [("dma_start", dict(out=LNP[:, 0, :], in_=g_ap.partition_broadcast(128))),
                                               ("dma_start", dict(out=LNP[:, 1, :], in_=b_ap.partition_broadcast(128)))])
        for t in range(NT):
            P.opm("vector", [b_h[t]], [b_ln], [
                ("bn_stats", dict(out=lnst[:, t, 0:6], in_=h[:, t, 0:512])),
                ("bn_stats", dict(out=lnst[:, t, 6:12], in_=h[:, t, 512:1024]))])
            P.op("vector", [b_ln], [b_ln], "bn_aggr", out=lnmv[:, t, :], in_=lnst[:, t, :])
        P.op("vector", [b_ln], [b_ln], "tensor_scalar", out=lnve[:], in0=lnmv[:, :, 1], scalar1=LN_EPS, scalar2=None, op0=ALU.add)
        P.op("scalar", [b_ln], [b_ln], "activation", out=lnve[:], in_=lnve[:], func=AF.Sqrt)
        P.op("vector", [b_ln], [b_ln], "reciprocal", out=lnrs[:], in_=lnve[:])
        for t in range(NT):
            P.op("vector", [b_ln, b_h[t]], [b_h[t]], "tensor_scalar", out=h[:, t, :], in0=h[:, t, :], scalar1=lnmv[:, t, 0:1],
                 scalar2=lnrs[:, t:t + 1], op0=ALU.subtract, op1=ALU.mult)
            P.op("gpsimd", [b_lnp, b_h[t]], [b_h[t]], "tensor_tensor", out=h[:, t, :], in0=h[:, t, :], in1=LNP[:, gi, :], op=ALU.mult)
            P.op("gpsimd", [b_lnp, b_h[t]], [b_h[t]], "tensor_tensor", out=h[:, t, :], in0=h[:, t, :], in1=LNP[:, bi, :], op=ALU.add)

    evac_n = [0]

    def evac(reads, writes, out, in_, scale=None):
        evac_n[0] += 1
        if scale is not None or evac_n[0] % 2:
            kw = dict(out=out, in_=in_, func=AF.Copy)
            if scale is not None:
                kw["scale"] = scale
            P.op("scalar", reads, writes, "activation", **kw)
        else:
            P.op("vector", reads, writes, "tensor_copy", out=out, in_=in_)

    def build_hT(scale_alpha):
        for t in range(NT):
            bk = (t % 2) * 2
            pT = ps[:, bk:bk + 2, :].rearrange("p a (c n) -> p (a c) n", n=128)
            P.opm("tensor", [b_h[t], b_const], [b_ps[bk], b_ps[bk + 1]], [
                ("transpose", dict(out=pT[:, c, :], in_=h[:, t, c * 128:(c + 1) * 128], identity=identF[:])) for c in range(8)])
            evac([b_ps[bk], b_ps[bk + 1]], [b_hT[t]], hT[:, :, t * 128:(t + 1) * 128], pT)
            if scale_alpha:
                P.op("gpsimd", [b_h[t]], [b_h[t]], "tensor_scalar", out=h[:, t, :], in0=h[:, t, :], scalar1=ALPHA, scalar2=None, op0=ALU.mult)

    s_w = [P.dsem() for _ in range(4)]
    s_ring = [P.dsem() for _ in range(RING)]

    s_h = P.dsem(); s_st = P.dsem()
    P.start_body()
    for s in range(1):
        def hin(t0, t1):
            return lambda sv: h_in[sv].rearrange("(t p) d -> p t d", p=128)[:, t0:t1, :]

        def hout(t0, t1):
            return lambda sv: h_out[sv].rearrange("(t p) d -> p t d", p=128)[:, t0:t1, :]
        P.dma("sync", s_h, [], b_h, [("dma_start", dict(out=h[:, t0:t0 + 4, :], in_=hin(t0, t0 + 4))) for t0 in (0, 4, 8, 12)]
              + [("dma_start", dict(out=h[:, 16:17, :], in_=hin(16, 17)))])
        if do_ln_in:
            layernorm(lnin_g, lnin_b)
        for l in range(NL):
            w = W[l]
            winv = w["winb"].rearrange("(c p) n -> p c n", p=128)
            wov = w["woutb"].rearrange("(c p) n -> p c n", p=128)
            wqv = w["wqb"].rearrange("(c p) n -> p c n", p=128)

            build_hT(True)
            P.barrier()
            A.reset()
            wsl = [A.alloc([8, 3, 128], BF16) for _ in range(2)]
            wo = [A.alloc([1, 1024], BF16)[:, 0, :] for _ in range(2)]
            b_wsl = [Buf(), Buf()]
            XT = A.alloc([1, LP], BF16)[:, 0, :]
            b_XT = Buf()
            qT = A.alloc([1, LP], BF16)[:, 0, :]
            kT = A.alloc([1, LP], BF16)[:, 0, :]
            Vh = A.alloc([NT, 130], BF16)
            b_q = Buf(); b_k = Buf(); b_v = Buf()
            Eb = [A.alloc([2, QC * 128], BF16) for _ in range(2)]
            b_E = [Buf(), Buf()]
            Gc = A.alloc([1, LP + 2], F32)[:, 0, :]
            zs = A.alloc([1, 512], F32)[:, 0, :]
            yt = A.alloc([1, 512], F32)[:, 0, :]
            cw = A.alloc([4, 3], F32)
            Oh = A.alloc([NT, 128], F32)
            ssq = A.alloc([1, NT], F32)[:, 0, :]
            rr = A.alloc([1, 4], F32)[:, 0, :]
            t1 = A.alloc([1, 128], F32)[:, 0, :]
            jk = A.alloc([1, 128], F32)[:, 0, :]
            onb = A.alloc([1, 128], BF16)[:, 0, :]
            b_G = Buf(); b_zs = Buf(); b_y = Buf(); b_cw = Buf(); b_O = Buf(); b_sm = Buf()

            P.dma("sync", s_cw, [b_cw], [b_cw], [("dma_start", dict(out=cw, in_=w["convw"].rearrange("(c p) j -> p c j", p=128)))])
            P.op("vector", [], [b_v], "memset", ap=Vh[:, :, 128:129], constant=1.0)
            P.op("vector", [], [b_G], "memset", ap=Gc[:, 0:2], constant=0.0)

            def load_w(j, cols, worow):
                i = j % 2
                P.dma("sync", s_w[i], [b_wconv[l]], [b_wsl[i]], [
                    ("dma_start", dict(out=wsl[i][:, :, k, :], in_=winv[:, :, cols[k]:cols[k] + 128])) for k in range(3)]
                    + [("dma_start", dict(out=wo[i], in_=wov[:, worow, :]))])

            def contribute(j):
                i = j % 2
                for t in range(NT):
                    P.opm("tensor", [b_XT, b_wsl[i]], [b_ps[0], b_ps[1]], [
                        ("matmul", dict(out=ps[:, 0, :], lhsT=XT[:, t * 128:(t + 1) * 128], rhs=wo[i][:, 0:512], start=True, stop=True)),
                        ("matmul", dict(out=ps[:, 1, :], lhsT=XT[:, t * 128:(t + 1) * 128], rhs=wo[i][:, 512:1024], start=True, stop=True))])
                    P.op("vector", [b_ps[0], b_ps[1], b_h[t]], [b_h[t]], "tensor_tensor", out=h[:, t, :], in0=h[:, t, :],
                         in1=ps[:, 0:2, :].rearrange("p a n -> p (a n)"), op=ALU.add)

            tchunks = [(c0, min(c0 + 512, LP)) for c0 in range(0, LP, 512)]
            jobs = [("conv", c) for c in range(4)] + [("attn", hh) for hh in range(4)]

            def job_cols(job):
                kind, c = job
                if kind == "conv":
                    return [1536 + c * 128, 2048 + c * 128, 2560 + c * 128], 4 + c
                return [c * 128, 512 + c * 128, 1024 + c * 128], c

            load_w(0, *job_cols(jobs[0]))
            for j, job in enumerate(jobs):
                i = j % 2
                if j + 1 < len(jobs):
                    load_w(j + 1, *job_cols(jobs[j + 1]))
                kind, c = job
                if kind == "conv":
                    for (c0, c1) in tchunks:
                        n = c1 - c0
                        tl = list(range(c0 // 128, c1 // 128))
                        for k in range(3):
                            P.opm("tensor", [b_hT[t] for t in tl] + [b_wsl[i]], [b_ps[2 + k]], [
                                ("matmul", dict(out=ps[:, 2 + k, 0:n], lhsT=wsl[i][:, dc, k, :], rhs=hT[:, dc, c0:c1],
                                                start=(dc == 0), stop=(dc == 7))) for dc in range(8)])
                        P.op("scalar", [b_ps[4]], [b_zs], "activation", out=zs[:, 0:n], in_=ps[:, 4, 0:n], func=AF.Copy)
                        P.op("vector", [b_ps[3], b_zs], [b_G], "tensor_tensor", out=Gc[:, 2 + c0:2 + c1], in0=ps[:, 3, 0:n], in1=zs[:, 0:n], op=ALU.mult)
                        P.op("vector", [b_G, b_cw], [b_y], "tensor_scalar", out=yt[:, 0:n], in0=Gc[:, c0:c1], scalar1=cw[:, c, 0:1], scalar2=None, op0=ALU.mult)
                        P.op("vector", [b_G, b_cw, b_y], [b_y], "scalar_tensor_tensor", out=yt[:, 0:n], in0=Gc[:, c0 + 1:c1 + 1], scalar=cw[:, c, 1:2],
                             in1=yt[:, 0:n], op0=ALU.mult, op1=ALU.add)
                        P.op("vector", [b_G, b_cw, b_y], [b_y], "scalar_tensor_tensor", out=yt[:, 0:n], in0=Gc[:, c0 + 2:c1 + 2], scalar=cw[:, c, 2:3],
                             in1=yt[:, 0:n], op0=ALU.mult, op1=ALU.add)
                        P.op("vector", [b_y, b_ps[2]], [b_XT], "tensor_tensor", out=XT[:, c0:c1], in0=yt[:, 0:n], in1=ps[:, 2, 0:n], op=ALU.mult)
                    contribute(j)
                    continue
                hh = c
                for (c0, c1) in tchunks:
                    n = c1 - c0
                    tl = list(range(c0 // 128, c1 // 128))
                    for k, (dst, bd) in enumerate(((qT, b_q), (kT, b_k))):
                        P.opm("tensor", [b_hT[t] for t in tl] + [b_wsl[i]], [b_ps[k]], [
                            ("matmul", dict(out=ps[:, k, 0:n], lhsT=wsl[i][:, dc, k, :], rhs=hT[:, dc, c0:c1],
                                            start=(dc == 0), stop=(dc == 7))) for dc in range(8)])
                        evac([b_ps[k]], [bd], dst[:, c0:c1], ps[:, k, 0:n], scale=(0.125 if k == 0 else None))
                for t in range(NT):
                    bk = t % 2
                    P.opm("tensor", [b_hT[t], b_wsl[i]], [b_ps[bk]], [
                        ("matmul", dict(out=ps[:, bk, 0:128], lhsT=hT[:, dc, t * 128:(t + 1) * 128], rhs=wsl[i][:, dc, 2, :],
                                        start=(dc == 0), stop=(dc == 7))) for dc in range(8)])
                    evac([b_ps[bk]], [b_v], Vh[:, t, 0:128], ps[:, bk, 0:128])
                steps = []
                for ci, qa in enumerate(range(0, NT, QC)):
                    qb = min(qa + QC, NT)
                    for kj in range(0, qb):
                        steps.append((ci, qa, qb, kj))

                def part_a(n):
                    ci, qa, qb, kj = steps[n]
                    e = n % 2
                    q_lo = max(qa, kj)
                    col0 = (q_lo - qa) * 128
                    col1 = (qb - qa) * 128
                    sb = 2 + 2 * e
                    for m in range(2):
                        P.op("tensor", [b_q, b_k], [b_ps[sb + m]], "matmul", out=ps[:, sb + m, col0:col1],
                             lhsT=kT[m * 64:(m + 1) * 64, kj * 128:(kj + 1) * 128], rhs=qT[m * 64:(m + 1) * 64, qa * 128 + col0:qa * 128 + col1],
                             start=True, stop=True)
                    for m in range(2):
                        P.op("scalar", [b_ps[sb + m], b_const], [b_E[e]], "activation", out=Eb[e][:, m, col0:col1], in_=ps[:, sb + m, col0:col1],
                             func=AF.Exp, bias=rb[:, 124 + hh:125 + hh], scale=1.0)
                    for qi in range(q_lo, qb):
                        off = qi - kj
                        if off > 1:
                            continue
                        ql = qi - qa
                        P.op("vector", [b_E[e], b_M], [b_E[e]], "tensor_tensor", out=Eb[e][:, :, ql * 128:(ql + 1) * 128],
                             in0=Eb[e][:, :, ql * 128:(ql + 1) * 128],
                             in1=Mt[:, hh, off * 128:(off + 1) * 128].unsqueeze(1).to_broadcast([128, 2, 128]), op=ALU.mult)

                def part_b(n):
                    ci, qa, qb, kj = steps[n]
                    e = n % 2
                    ab = 6 if ci % 2 == 0 else 0
                    q_lo = max(qa, kj)
                    insts = []
                    for qi in range(q_lo, qb):
                        ql = qi - qa
                        for m in range(2):
                            insts.append(("matmul", dict(out=ps[:, ab + m, ql * 129:(ql + 1) * 129], lhsT=Eb[e][:, m, ql * 128:(ql + 1) * 128],
                                                         rhs=Vh[:, kj, 0:129], start=(kj == 0 and ql == 0), stop=(kj == qi),
                                                         skip_group_check=True)))
                    P.opm("tensor", [b_E[e], b_v], [b_ps[ab], b_ps[ab + 1]], insts)
                    if kj != qb - 1:
                        return
                    for qi in range(qa, qb):
                        ql = qi - qa
                        for m in range(2):
                            P.op("vector", [b_ps[ab + m]], [b_sm], "reciprocal", out=rr[:, m:m + 1], in_=ps[:, ab + m, ql * 129 + 128:ql * 129 + 129])
                        P.op("vector", [b_sm, b_lam], [b_sm], "tensor_tensor", out=rr[:, 2:3], in0=rr[:, 1:2], in1=neglam[:, l:l + 1], op=ALU.mult)
                        P.op("vector", [b_sm, b_ps[ab + 1]], [b_sm], "tensor_scalar", out=t1, in0=ps[:, ab + 1, ql * 129:ql * 129 + 128], scalar1=rr[:, 2:3],
                             scalar2=None, op0=ALU.mult)
                        P.op("vector", [b_sm, b_ps[ab]], [b_O], "scalar_tensor_tensor", out=Oh[:, qi, :], in0=ps[:, ab, ql * 129:ql * 129 + 128],
                             scalar=rr[:, 0:1], in1=t1, op0=ALU.mult, op1=ALU.add)
                        P.op("vector", [b_O], [b_sm], "scalar_tensor_tensor", out=jk, in0=Oh[:, qi, :], scalar=1.0, in1=Oh[:, qi, :],
                             op0=ALU.mult, op1=ALU.mult, accum_out=ssq[:, qi:qi + 1])

                part_a(0)
                for n in range(len(steps)):
                    if n + 1 < len(steps):
                        part_a(n + 1)
                    part_b(n)
                P.op("vector", [b_sm], [b_sm], "tensor_scalar", out=ssq, in0=ssq, scalar1=1.0 / 128.0, scalar2=RMS_EPS, op0=ALU.mult, op1=ALU.add)
                P.op("scalar", [b_sm], [b_sm], "activation", out=ssq, in_=ssq, func=AF.Sqrt)
                P.op("vector", [b_sm], [b_sm], "reciprocal", out=ssq, in_=ssq)
                for t in range(NT):
                    P.op("vector", [b_O, b_sm], [b_sm], "tensor_scalar", out=onb, in0=Oh[:, t, :], scalar1=ssq[:, t:t + 1], scalar2=None, op0=ALU.mult)
                    bk = t % 2
                    pTb = ps[:, bk, 0:64].bitcast(BF16)
                    P.op("tensor", [b_sm, b_const], [b_ps[bk]], "transpose", out=pTb, in_=onb, identity=identB[:])
                    P.op("scalar", [b_ps[bk], b_lam], [b_XT], "activation", out=XT[:, t * 128:(t + 1) * 128], in_=pTb, func=AF.Copy, scale=gsc[:, l:l + 1])
                contribute(j)

            layernorm(w["ln1g"], w["ln1b"])
            build_hT(False)
            P.barrier()

            A.reset()
            IDX = A.alloc([1, LP], I32)[:, 0, :]
            GT = A.alloc([1, LP], F32)[:, 0, :]
            kyb = A.alloc([16, 128], BF16)
            wqs = [A.alloc([8, 128], BF16) for _ in range(2)]
            qTp = A.alloc([16, 256], BF16)
            S = A.alloc([16, 128], F32)
            SCR = A.alloc([16, 128], F32)
            OH = SCR[:].rearrange("p g n -> p (g n)").rearrange("p (h k i) -> p h k i", h=8, k=16)
            V1 = A.alloc([16, 16], F32)
            I1 = A.alloc([16, 16], U32)
            I1f = A.alloc([16, 16], F32)
            CAND = A.alloc([8, 256], F32)
            CSCR = A.alloc([8, 256], F32)
            SC = A.alloc([8, 16], F32)
            CI = A.alloc([8, 16], U32)
            CIh = A.alloc([8, 16], U32)
            CIl = A.alloc([8, 16], U32)
            CIhf = A.alloc([8, 16], F32)
            CIlf = A.alloc([8, 16], F32)
            E1 = A.alloc([8, 16], F32)
            E2 = A.alloc([8, 16], F32)
            EIDX = A.alloc([8, 16], F32)
            G16 = A.alloc([8, 16], F32)
            sm8 = A.alloc([8], F32)
            d1_end = A.off
            b_ky = Buf(); b_wq = [Buf(), Buf()]; b_qp = Buf(); b_S = Buf(); b_tk = Buf(); b_idx = Buf(); b_gt = Buf()
            P.dma("sync", s_ky, [b_ky], [b_ky], [("dma_start", dict(out=kyb, in_=w["keysb"].rearrange("(g d) n -> d g n", d=128)))])
            nwq = 0
            for c0 in range(0, LP, 256):
                c1 = min(c0 + 256, LP)
                n = c1 - c0
                tl = list(range(c0 // 128, c1 // 128))
                for g in range(16):
                    i = nwq % 2
                    nwq += 1
                    P.dma("sync", s_w[2 + i], [b_wconv[l]], [b_wq[i]], [("dma_start", dict(out=wqs[i], in_=wqv[:, :, g * 128:(g + 1) * 128]))])
                    bk = g % 2
                    P.opm("tensor", [b_hT[t] for t in tl] + [b_wq[i]], [b_ps[bk]], [
                        ("matmul", dict(out=ps[:, bk, 0:n], lhsT=wqs[i][:, dc, :], rhs=hT[:, dc, c0:c1], start=(dc == 0), stop=(dc == 7)))
                        for dc in range(8)])
                    evac([b_ps[bk]], [b_qp], qTp[:, g, 0:n], ps[:, bk, 0:n])
                for t in tl:
                    tc0 = t * 128 - c0
                    P.opm("tensor", [b_qp, b_ky], [b_ps[4], b_ps[5], b_ps[6], b_ps[7]], [
                        ("matmul", dict(out=ps[:, 4 + g // 4, (g % 4) * 128:(g % 4 + 1) * 128], lhsT=qTp[:, g, tc0:tc0 + 128], rhs=kyb[:, g, :],
                                        start=True, stop=True, skip_group_check=True)) for g in range(16)])
                    P.op("scalar", [b_ps[4], b_ps[5], b_ps[6], b_ps[7]], [b_S], "activation", out=S[:].rearrange("p g n -> p (g n)"),
                         in_=ps[:, 4:8, :].rearrange("p a n -> p (a n)"), func=AF.Copy)
                    bg = [[Buf() for _ in range(16)] for _ in range(4)]
                    for g in range(16):
                        P.op("vector", [b_S], [bg[0][g]], "max", out=V1[:, g, 0:8], in_=S[:, g, :])
                    for g in range(16):
                        P.op("vector", [b_S, bg[0][g]], [bg[1][g]], "match_replace", out=SCR[:, g, :], in_to_replace=V1[:, g, 0:8], in_values=S[:, g, :], imm_value=-1e30)
                    for g in range(16):
                        P.op("vector", [bg[1][g]], [bg[2][g]], "max", out=V1[:, g, 8:16], in_=SCR[:, g, :])
                    for g in range(16):
                        P.op("vector", [b_S, bg[0][g]], [bg[3][g]], "max_index", out=I1[:, g, 0:8], in_max=V1[:, g, 0:8], in_values=S[:, g, :])
                    for g in range(16):
                        P.op("vector", [bg[1][g], bg[2][g]], [bg[3][g]], "max_index", out=I1[:, g, 8:16], in_max=V1[:, g, 8:16], in_values=SCR[:, g, :])
                    allg = [x for row in bg for x in row]
                    P.op("vector", allg + [b_tk], [b_tk], "tensor_copy", out=I1f[:], in_=I1[:])
                    I1v = I1f[:].rearrange("p (h two) k -> p h two k", two=2)
                    V1v = V1[:].rearrange("p (h two) k -> p h two k", two=2)
                    P.op("vector", [b_tk], [b_tk], "tensor_scalar", out=I1v[:, :, 0, :], in0=I1v[:, :, 0, :], scalar1=128.0, scalar2=None, op0=ALU.mult)
                    C4 = CAND[:].rearrange("p h (i j) -> p h i j", j=16)
                    P.op("vector", allg + [b_tk], [b_tk], "tensor_tensor", out=C4, in0=V1v[:, :, 0, :].unsqueeze(3).to_broadcast([128, 8, 16, 16]),
                         in1=V1v[:, :, 1, :].unsqueeze(2).to_broadcast([128, 8, 16, 16]), op=ALU.add)
                    bh = [[Buf() for _ in range(8)] for _ in range(4)]
                    for hd in range(8):
                        P.op("vector", [b_tk], [bh[0][hd]], "max", out=SC[:, hd, 0:8], in_=CAND[:, hd, :])
                    for hd in range(8):
                        P.op("vector", [b_tk, bh[0][hd]], [bh[1][hd]], "match_replace", out=CSCR[:, hd, :], in_to_replace=SC[:, hd, 0:8], in_values=CAND[:, hd, :], imm_value=-1e30)
                    for hd in range(8):
                        P.op("vector", [bh[1][hd]], [bh[2][hd]], "max", out=SC[:, hd, 8:16], in_=CSCR[:, hd, :])
                    for hd in range(8):
                        P.op("vector", [b_tk, bh[0][hd]], [bh[3][hd]], "max_index", out=CI[:, hd, 0:8], in_max=SC[:, hd, 0:8], in_values=CAND[:, hd, :])
                    for hd in range(8):
                        P.op("vector", [bh[1][hd], bh[2][hd]], [bh[3][hd]], "max_index", out=CI[:, hd, 8:16], in_max=SC[:, hd, 8:16], in_values=CSCR[:, hd, :])
                    allh = [x for row in bh for x in row]
                    P.op("vector", allh + [b_tk], [b_tk], "tensor_scalar", out=CIh[:], in0=CI[:], scalar1=4, scalar2=None, op0=ALU.logical_shift_right)
                    P.op("vector", [b_tk], [b_tk], "tensor_scalar", out=CIl[:], in0=CI[:], scalar1=15, scalar2=None, op0=ALU.bitwise_and)
                    P.op("vector", [b_tk], [b_tk], "tensor_copy", out=CIhf[:], in_=CIh[:])
                    P.op("vector", [b_tk], [b_tk], "tensor_copy", out=CIlf[:], in_=CIl[:])
                    io4 = iota16[:].unsqueeze(1).unsqueeze(1).to_broadcast([128, 8, 16, 16])
                    for (cif, two, eo) in ((CIhf, 0, E1), (CIlf, 1, E2)):
                        P.op("vector", [b_tk, b_const], [b_tk], "tensor_tensor", out=OH, in0=cif[:].unsqueeze(3).to_broadcast([128, 8, 16, 16]), in1=io4, op=ALU.is_equal)
                        P.op("vector", [b_tk], [b_tk], "tensor_tensor", out=OH, in0=OH, in1=I1v[:, :, two, :].unsqueeze(2).to_broadcast([128, 8, 16, 16]), op=ALU.mult)
                        P.op("vector", [b_tk], [b_tk], "tensor_reduce", out=eo[:], in_=OH, axis=AX.X, op=ALU.add)
                    P.op("vector", [b_tk], [b_tk], "tensor_tensor", out=EIDX[:], in0=E1[:], in1=E2[:], op=ALU.add)
                    P.op("vector", [b_tk], [b_tk], "tensor_scalar", out=EIDX[:], in0=EIDX[:], scalar1=float(NE - 1), scalar2=None, op0=ALU.min)
                    P.op("vector", [b_tk], [b_tk], "tensor_tensor", out=G16[:], in0=SC[:], in1=SC[:, :, 0:1].to_broadcast([128, 8, 16]), op=ALU.subtract)
                    P.op("scalar", [b_tk], [b_tk], "activation", out=G16[:], in_=G16[:], func=AF.Exp)
                    P.op("vector", [b_tk], [b_tk], "tensor_reduce", out=sm8, in_=G16[:], axis=AX.X, op=ALU.add)
                    P.op("vector", [b_tk], [b_tk], "reciprocal", out=sm8, in_=sm8)
                    P.op("vector", [b_tk], [b_tk], "tensor_tensor", out=G16[:], in0=G16[:], in1=sm8.unsqueeze(2).to_broadcast([128, 8, 16]), op=ALU.mult)
                    P.opm("tensor", [b_tk, b_const], [b_ps[2], b_ps[3]], [
                        ("transpose", dict(out=ps[:, 2, 0:128], in_=EIDX[:].rearrange("p h k -> p (h k)"), identity=identF[:])),
                        ("transpose", dict(out=ps[:, 3, 0:128], in_=G16[:].rearrange("p h k -> p (h k)"), identity=identF[:]))])
                    P.op("vector", [b_ps[2]], [b_idx], "tensor_copy", out=IDX[:, t * 128:(t + 1) * 128], in_=ps[:, 2, 0:128])
                    P.op("scalar", [b_ps[3]], [b_gt], "activation", out=GT[:, t * 128:(t + 1) * 128], in_=ps[:, 3, 0:128], func=AF.Copy)
            P.barrier()

            A.off = (2 * LP * 4 + 63) // 64 * 64
            hb = [A.alloc([D], BF16) for _ in range(2)]
            a0 = A.alloc([2, 128], F32)
            gel = A.alloc([2, 128], F32)
            Wg = A.alloc([2, 128], F32)
            junk = A.alloc([D], BF16)
            Lt = A.alloc([4, 128], BF16)
            b_hb = [Buf(), Buf()]; b_ring = [Buf() for _ in range(RING)]
            b_a0 = [Buf() for _ in range(4)]; b_gl = [Buf() for _ in range(4)]; b_Lt = [Buf() for _ in range(4)]; b_W = [Buf() for _ in range(4)]
            toks = [(t, p) for t in range(NT) for p in range(128 if t < 16 else 16)]
            NTOK = len(toks)
            KSK = 2

            def gather(gi):
                t, p = toks[gi]
                sl = gi % RING
                P.dma("gpsimd", s_ring[sl], [b_idx], [b_ring[sl]], [("indirect_dma_start", dict(
                    out=ring[:, sl, :], out_offset=None, in_=w["uvb"],
                    in_offset=bass.IndirectOffsetOnAxis(ap=IDX[:, t * 128 + p:t * 128 + p + 1], axis=0),
                    bounds_check="NEREG", oob_is_err=False))])

            for gi in range(min(RING, NTOK)):
                gather(gi)
            KLT = 2
            KAC = 3
            for i in range(NTOK + KAC):
                j = i - KLT
                if 0 <= j < NTOK:
                    t, p = toks[j]
                    P.op("vector", [b_gl[j % 4], b_gt, b_const], [b_Lt[j % 4]], "tensor_scalar", out=Lt[:, j % 4, :],
                         in0=zmask[:, 127 - p:255 - p], scalar1=gel[:, t % 2, p:p + 1], scalar2=GT[:, t * 128 + p:t * 128 + p + 1],
                         op0=ALU.mult, op1=ALU.mult)
                if i < NTOK:
                    t, p = toks[i]
                    hi = t % 2
                    sl = i % RING
                    kb = (i % 2) * 2
                    if p == 0:
                        P.op("scalar", [b_h[t]], [b_hb[hi]], "activation", out=hb[hi], in_=h[:, t, :], func=AF.Copy)
                    P.opm("tensor", [b_hb[hi], b_const], [b_ps[kb], b_ps[kb + 1]], [
                        ("matmul", dict(out=ps[:, kb + hf, :], lhsT=identB[:, p:p + 1].to_broadcast([128, 128]),
                                        rhs=hb[hi][:, hf * 512:(hf + 1) * 512], start=True, stop=True)) for hf in range(2)])
                    P.op("vector", [b_ps[kb], b_ps[kb + 1], b_ring[sl]], [b_a0[i % 4]], "scalar_tensor_tensor", out=junk,
                         in0=ring[:, sl, 0:1024], scalar=1.0, in1=ps[:, kb:kb + 2, :].rearrange("p a n -> p (a n)"),
                         op0=ALU.mult, op1=ALU.mult, accum_out=a0[:, hi, p:p + 1])
                    P.op("scalar", [b_a0[i % 4]], [b_gl[i % 4]], "activation", out=gel[:, hi, p:p + 1], in_=a0[:, hi, p:p + 1], func=AF.Gelu)
                j = i - KAC
                if j >= 0:
                    t, p = toks[j]
                    npt = 128 if t < 16 else 16
                    sl = j % RING
                    accb = 4 + 2 * (t % 2)
                    P.opm("tensor", [b_Lt[j % 4], b_ring[sl]], [b_ps[accb], b_ps[accb + 1]], [
                        ("matmul", dict(out=ps[:, accb, :], lhsT=Lt[:, j % 4, :], rhs=ring[:, sl, 1024:1536], start=(p == 0), stop=(p == npt - 1))),
                        ("matmul", dict(out=ps[:, accb + 1, :], lhsT=Lt[:, j % 4, :], rhs=ring[:, sl, 1536:2048], start=(p == 0), stop=(p == npt - 1)))])
                    if p == npt - 1:
                        P.op("vector", [b_ps[accb], b_ps[accb + 1], b_h[t]], [b_h[t]], "scalar_tensor_tensor", out=h[:, t, :], in0=h[:, t, :],
                             scalar=ALPHA, in1=ps[:, accb:accb + 2, :].rearrange("p a n -> p (a n)"), op0=ALU.mult, op1=ALU.add)
                    if j + RING < NTOK:
                        gather(j + RING)
            P.barrier()
            layernorm(w["ln2g"], w["ln2b"])
        P.dma("sync", s_st, b_h, [], [("dma_start", dict(out=hout(t0, t0 + 4), in_=h[:, t0:t0 + 4, :])) for t0 in (0, 4, 8, 12)]
              + [("dma_start", dict(out=hout(16, 17), in_=h[:, 16:17, :]))])
    P._wait("sync", (s_st.sem, s_st.cnt))
    P.iter_sync(None)
    P.emit(NSEQ)
    P.stack.close()
    return nc


_CACHE = {}


def _get_nc(NL, do_ln_in):
    key = (NL, do_ln_in)
    if key not in _CACHE:
        _CACHE[key] = build(NL, do_ln_in)
    return _CACHE[key]


def kernel(x, meta_tokens, ln_in_g, ln_in_b, rel_bias, w_in, conv_w, lambda_q1, lambda_k1, lambda_q2, lambda_k2,
           subln_g, w_out, ln1_g, ln1_b, peer_w_q, peer_sub_keys, peer_u, peer_v, ln2_g, ln2_b):
    f = lambda a: np.ascontiguousarray(np.asarray(a, dtype=np.float32))
    x = f(x)
    B = x.shape[0]
    ncore = 8
    h0 = np.zeros((B, LP, D), np.float32)
    h0[:, :NMETA] = f(meta_tokens)[None]
    h0[:, NMETA:L] = x
    identF = np.eye(128, dtype=np.float32)
    identB = np.eye(128).astype(ml_dtypes.bfloat16)
    zmask = np.zeros((128, 255), np.float32); zmask[:, 127] = 1.0
    dist = (np.arange(256)[None, :] - np.arange(128)[:, None]).astype(np.float32)
    keysT = np.ascontiguousarray(np.transpose(f(peer_sub_keys), (0, 1, 2, 4, 3))).reshape(DEPTH, 2048, 128)
    convwT = np.ascontiguousarray(np.transpose(f(conv_w), (0, 2, 1)))
    lam_init = np.array([0.8 - 0.6 * math.exp(-0.3 * l) for l in range(DEPTH)], np.float32)

    def layer_inputs(l, j):
        return {f"w_in{j}": f(w_in[l]), f"w_out{j}": f(w_out[l]), f"w_q{j}": f(peer_w_q[l]), f"keysT{j}": keysT[l],
                f"u{j}": f(peer_u[l]), f"v{j}": f(peer_v[l]), f"convw{j}": convwT[l], f"lq1_{j}": f(lambda_q1[l]),
                f"lk1_{j}": f(lambda_k1[l]), f"lq2_{j}": f(lambda_q2[l]), f"lk2_{j}": f(lambda_k2[l]), f"subg{j}": f(subln_g[l]),
                f"ln1g{j}": f(ln1_g[l]), f"ln1b{j}": f(ln1_b[l]), f"ln2g{j}": f(ln2_g[l]), f"ln2b{j}": f(ln2_b[l])}

    iota16 = np.ascontiguousarray(np.broadcast_to(np.arange(16, dtype=np.float32)[None, :], (128, 16)))
    common = dict(lnin_g=f(ln_in_g), lnin_b=f(ln_in_b), rel_bias=f(rel_bias), identF=identF, identB=identB, zmask=zmask, dist=dist, iota16=iota16)
    hcur = [np.ascontiguousarray(h0[c * NSEQ:(c + 1) * NSEQ]) for c in range(ncore)]
    NLL = LAYERS_PER_LAUNCH
    for l0 in range(0, DEPTH, NLL):
        nc = _get_nc(NLL, l0 == 0)
        base = dict(common)
        base["laminit"] = lam_init[l0:l0 + NLL].copy()
        for j in range(NLL):
            base.update(layer_inputs(l0 + j, j))
        in_maps = []
        for c in range(ncore):
            m = dict(base)
            m["h_in"] = hcur[c]
            in_maps.append(m)
        res = run_bass_kernel_spmd(nc, in_maps, core_ids=list(range(ncore)))
        hcur = [np.asarray(r["h_out"]) for r in res.results]
    out = np.concatenate(hcur, axis=0)[:, NMETA:L, :]
    return np.ascontiguousarray(out.astype(np.float32))
```

```python
import contextlib
import math
import numpy as np
import ml_dtypes
import concourse.bass as bass
import concourse.mybir as mybir
from concourse.bass_utils import run_bass_kernel_spmd

F32 = mybir.dt.float32
BF16 = mybir.dt.bfloat16
I32 = mybir.dt.int32
U32 = mybir.dt.uint32
AF = mybir.ActivationFunctionType
ALU = mybir.AluOpType
AX = mybir.AxisListType

ENGS = ("sync", "scalar", "vector", "gpsimd", "tensor")

D = 1024
SEQ = 2048
NMETA = 16
L = SEQ + NMETA
NT = 17
LP = NT * 128
DEPTH = 4
NSEQ = 4
ALPHA = (2 * DEPTH) ** 0.25
LN_EPS = 1e-5
RMS_EPS = 1e-5
NE = 16384
QC = 3
RING = 8
NB = 4
LAYERS_PER_LAUNCH = 4


ALL_BUFS = []


class Buf:
    __slots__ = ("w", "r")

    def __init__(self):
        self.w = None
        self.r = {}
        ALL_BUFS.append(self)


class DSem:
    def __init__(self, sem):
        self.sem = sem
        self.cnt = 0


class Prog:
    def __init__(self, nc):
        self.nc = nc
        self.stack = contextlib.ExitStack()
        self.ops = {e: [] for e in ENGS}
        self.cnt = {e: 0 for e in ENGS}
        self.seen = {e: {} for e in ENGS}
        self.sem = {e: self.stack.enter_context(nc.semaphore("p_" + e)) for e in ENGS}
        self.nsem = 0
        self.nereg = None
        self.pre = None
        self.dsems = []
        self.loopvar = {}
        self.B1 = self.stack.enter_context(nc.semaphore("B1"))
        self.B2 = self.stack.enter_context(nc.semaphore("B2"))

    def sbuf(self, name, shape, dt):
        return self.stack.enter_context(self.nc.sbuf_tensor("sb_" + name, shape, dt))

    def psum(self, name, shape, dt):
        return self.stack.enter_context(self.nc.psum_tensor("ps_" + name, shape, dt))

    def dsem(self):
        self.nsem += 1
        d = DSem(self.stack.enter_context(self.nc.semaphore(f"d{self.nsem}")))
        self.dsems.append(d)
        return d

    def _wait(self, eng, tok):
        if tok is None:
            return
        sem, val = tok
        k = id(sem)
        if self.seen[eng].get(k, 0) >= val:
            return
        self.seen[eng][k] = val
        self.ops[eng].append(lambda e, sem=sem, val=val: e.wait_ge(sem, val))

    def _deps(self, eng, reads, writes):
        for b in reads:
            self._wait(eng, b.w)
        for b in writes:
            self._wait(eng, b.w)
            for t in b.r.values():
                self._wait(eng, t)

    def _commit(self, tok, reads, writes):
        k = id(tok[0])
        for b in reads:
            o = b.r.get(k)
            if o is None or o[1] < tok[1]:
                b.r[k] = tok
        for b in writes:
            b.w = tok
            b.r = {}

    def op(self, eng, reads, writes, name, **kw):
        return self.opm(eng, reads, writes, [(name, kw)])

    def opm(self, eng, reads, writes, insts):
        self._deps(eng, reads, writes)
        self.cnt[eng] += 1
        sem = self.sem[eng]
        insts = list(insts)

        def run(e, insts=insts, sem=sem):
            r = None
            for name, kw in insts:
                try:
                    r = getattr(e, name)(**kw)
                except Exception:
                    print("FAILED OP", name, {k: (getattr(v, "shape", v), getattr(v, "ap", None)) for k, v in kw.items()})
                    raise
            r.then_inc(sem, 1)
        self.ops[eng].append(run)
        tok = (sem, self.cnt[eng])
        self._commit(tok, reads, writes)
        return tok

    def dma(self, eng, ds, reads, writes, insts):
        self._deps(eng, reads, writes)
        sem = ds.sem
        insts = list(insts)

        def run(e, insts=insts, sem=sem, eng=eng):
            for name, kw in insts:
                if any(callable(v) for v in kw.values()):
                    kw = {k: (v(self.loopvar[eng]) if callable(v) else v) for k, v in kw.items()}
                if kw.get("bounds_check") == "NEREG":
                    if self.nereg is None:
                        self.nereg = e.to_reg(NE - 1)
                    kw = dict(kw)
                    kw["bounds_check"] = self.nereg
                try:
                    getattr(e, name)(**kw).then_inc(sem, 16)
                except Exception:
                    print("FAILED DMA", name, {k: (getattr(v, "shape", v), getattr(v, "ap", None)) for k, v in kw.items()})
                    raise
        self.ops[eng].append(run)
        ds.cnt += 16 * len(insts)
        tok = (sem, ds.cnt)
        self._commit(tok, reads, writes)
        return tok

    def barrier(self):
        toks = [(self.sem[e], self.cnt[e]) for e in ENGS if self.cnt[e] > 0]
        for e in ENGS:
            for t in toks:
                self._wait(e, t)

    def iter_sync(self, it):
        toks = [(self.sem[e], self.cnt[e]) for e in ENGS if self.cnt[e] > 0]
        toks += [(d.sem, d.cnt) for d in self.dsems if d.cnt > 0]
        allsems = [self.sem[e] for e in ENGS] + [d.sem for d in self.dsems]
        B1, B2 = self.B1, self.B2
        for en in ENGS:
            for t in toks:
                self._wait(en, t)
            if en == "sync":
                def run(e, it=it, allsems=allsems):
                    n = it if it is not None else (self.loopvar["sync"] + 2)
                    e.sem_inc(B1, 1)
                    e.wait_ge(B1, n * len(ENGS))
                    for sm in allsems:
                        e.sem_clear(sm)
                    e.drain().then_inc(B2, 1)
            else:
                def run(e, it=it, en=en):
                    n = it if it is not None else (self.loopvar[en] + 2)
                    if en == "gpsimd":
                        e.dma_reset()
                    e.sem_inc(B1, 1)
                    e.wait_ge(B2, n)
            self.ops[en].append(run)
        self.cnt = {e: 0 for e in ENGS}
        self.seen = {e: {} for e in ENGS}
        for d in self.dsems:
            d.cnt = 0
        for b in ALL_BUFS:
            b.w = None
            b.r = {}

    def start_body(self):
        self.iter_sync(1)
        self.pre = self.ops
        self.ops = {e: [] for e in ENGS}

    def emit(self, niter):
        with self.nc.Block() as block:
            for en in ENGS:
                pre = self.pre[en]
                ops = self.ops[en]

                def body(e, pre=pre, ops=ops, en=en):
                    for f in pre:
                        f(e)
                    with e.Fori(0, niter) as s:
                        self.loopvar[en] = s
                        for f in ops:
                            f(e)
                getattr(block, en)(body)


def _bucket_thresholds():
    n = np.arange(0, 4096, dtype=np.int32)
    nf = np.maximum(n, 1).astype(np.float32)
    large = 16 + (np.log(nf / np.float32(16)) / np.float32(math.log(128 / 16)) * np.float32(16)).astype(np.int32)
    large = np.minimum(large, 31)
    bucket = np.where(n < 16, n, large)
    thr = []
    for b in range(1, 32):
        idx = np.nonzero(bucket >= b)[0]
        thr.append(int(idx[0]))
    return thr


def build(NL, do_ln_in):
    nc = bass.Bass("TRN2", target_bir_lowering=False)

    def din(name, shape, dt=F32):
        return nc.dram_tensor(name, shape, dt, kind="ExternalInput").ap()

    def dint(name, shape, dt):
        return nc.dram_tensor(name, shape, dt, kind="Internal").ap()

    h_in = din("h_in", [NSEQ, LP, D])
    h_out = nc.dram_tensor("h_out", [NSEQ, LP, D], F32, kind="ExternalOutput").ap()
    lnin_g = din("lnin_g", [D]); lnin_b = din("lnin_b", [D])
    rel_bias = din("rel_bias", [32, 4])
    laminit = din("laminit", [NL])
    identF_d = din("identF", [128, 128]); identB_d = din("identB", [128, 128], BF16)
    zmask_d = din("zmask", [128, 255]); dist_d = din("dist", [128, 256]); iota_d = din("iota16", [128, 16])
    W = []
    for l in range(NL):
        W.append(dict(
            w_in=din(f"w_in{l}", [D, 3072]), w_out=din(f"w_out{l}", [D, D]), w_q=din(f"w_q{l}", [D, 2048]),
            keysT=din(f"keysT{l}", [2048, 128]), u=din(f"u{l}", [NE, D]), v=din(f"v{l}", [NE, D]),
            convw=din(f"convw{l}", [512, 3]), lq1=din(f"lq1_{l}", [64]), lk1=din(f"lk1_{l}", [64]),
            lq2=din(f"lq2_{l}", [64]), lk2=din(f"lk2_{l}", [64]), subg=din(f"subg{l}", [128]),
            ln1g=din(f"ln1g{l}", [D]), ln1b=din(f"ln1b{l}", [D]), ln2g=din(f"ln2g{l}", [D]), ln2b=din(f"ln2b{l}", [D]),
            winb=dint(f"winb{l}", [D, 3072], BF16), woutb=dint(f"woutb{l}", [D, D], BF16),
            wqb=dint(f"wqb{l}", [D, 2048], BF16), keysb=dint(f"keysb{l}", [2048, 128], BF16),
            uvb=dint(f"uvb{l}", [NE, 2048], BF16)))

    P = Prog(nc)
    h = P.sbuf("h", [128, NT, D], F32)
    X = P.sbuf("X", [128, 8 * LP], BF16)
    hT = X[:, :].rearrange("p (c t) -> p c t", c=8)
    ring = X[:, 0:RING * 2048].rearrange("p (s n) -> p s n", s=RING)
    LNP = P.sbuf("LNP", [128, 2, D], F32)
    identF = P.sbuf("identF", [128, 128], F32)
    identB = P.sbuf("identB", [128, 128], BF16)
    zmask = P.sbuf("zmask", [128, 255], F32)
    dist = P.sbuf("dist", [128, 256], F32)
    iota16 = P.sbuf("iota16", [128, 16], F32)
    rb = P.sbuf("rb", [128, 128], F32)
    dl = P.sbuf("dl", [128, 124], F32)
    negc = P.sbuf("negc", [128, 4], F32)
    Mt = P.sbuf("Mt", [128, 4, 256], BF16)
    lamt = P.sbuf("lamt", [128, 4 * NL + 4], F32)
    neglam = P.sbuf("neglam", [128, NL], F32)
    gsc = P.sbuf("gsc", [128, NL], F32)
    lnst = P.sbuf("lnst", [128, NT, 12], F32)
    lnmv = P.sbuf("lnmv", [128, NT, 2], F32)
    lnve = P.sbuf("lnve", [128, NT], F32)
    lnrs = P.sbuf("lnrs", [128, NT], F32)
    YB = 74 * 1024
    Y = P.sbuf("Y", [128, YB // 2], BF16)
    ps = P.psum("ps", [128, 8, 512], F32)
    b_ps = [Buf() for _ in range(8)]

    class Arena:
        def __init__(self):
            self.off = 0

        def reset(self):
            self.off = 0

        def alloc(self, shape, dt):
            n = int(np.prod(shape))
            esz = 2 if dt == BF16 else 4
            nb = (n * esz + 63) // 64 * 64
            assert self.off + nb <= YB, (self.off, nb)
            v = Y[:, self.off // 2:(self.off + n * esz) // 2]
            self.off += nb
            if dt != BF16:
                v = v.bitcast(dt)
            if len(shape) == 1:
                return v
            if len(shape) == 2:
                return v.rearrange("p (a b) -> p a b", a=shape[0])
            if len(shape) == 3:
                return v.rearrange("p (a b c) -> p a b c", a=shape[0], b=shape[1])
            raise ValueError
    A = Arena()

    b_h = [Buf() for _ in range(NT)]
    b_hT = [Buf() for _ in range(NT)]
    b_const = Buf()
    b_lnp = Buf()
    b_ln = Buf()
    ld = P.dsem()
    s_lnp = P.dsem()

    P.dma("sync", ld, [], [b_const], [
        ("dma_start", dict(out=identF[:], in_=identF_d)), ("dma_start", dict(out=identB[:], in_=identB_d)),
        ("dma_start", dict(out=zmask[:], in_=zmask_d)), ("dma_start", dict(out=dist[:], in_=dist_d)),
        ("dma_start", dict(out=iota16[:], in_=iota_d)),
        ("dma_start", dict(out=rb[:], in_=rel_bias.rearrange("a b -> (a b)").partition_broadcast(128))),
        ("dma_start", dict(out=lamt[:, 4 * NL:4 * NL + NL], in_=laminit.partition_broadcast(128)))])

    A.reset()
    NSTG = 4
    stg_f = [A.alloc([1, 2048], F32) for _ in range(NSTG)]
    stg_b = [A.alloc([1, 2048], BF16) for _ in range(NSTG)]
    b_sf = [Buf() for _ in range(NSTG)]; b_sb = [Buf() for _ in range(NSTG)]
    s_in = [P.dsem() for _ in range(NSTG)]; s_out = [P.dsem() for _ in range(NSTG)]
    cvn = [0]
    b_wconv = [Buf() for _ in range(NL)]
    cv_jobs = []

    def conv_chunk(src_ap, dst_ap, shp, l):
        cv_jobs.append((src_ap, dst_ap, shp))

    def cv_views(i, shp):
        n = int(np.prod(shp))
        sf = stg_f[i][:, 0, 0:n]; sb = stg_b[i][:, 0, 0:n]
        if len(shp) == 2:
            return sf, sb, sf.rearrange("p (a c) -> p a c", a=shp[0]), sb.rearrange("p (a c) -> p a c", a=shp[0])
        return sf, sb, sf, sb

    def cv_load(n):
        src_ap, dst_ap, shp = cv_jobs[n]
        i = n % NSTG
        sf, sb, sfv, sbv = cv_views(i, shp)
        P.dma("sync", s_in[i], [], [b_sf[i]], [("dma_start", dict(out=sfv, in_=src_ap))])

    def cv_cast_store(n):
        src_ap, dst_ap, shp = cv_jobs[n]
        i = n % NSTG
        sf, sb, sfv, sbv = cv_views(i, shp)
        if n % 2:
            P.op("scalar", [b_sf[i]], [b_sb[i]], "activation", out=sb, in_=sf, func=AF.Copy)
        else:
            P.op("vector", [b_sf[i]], [b_sb[i]], "tensor_copy", out=sb, in_=sf)
        P.dma("sync", s_out[i], [b_sb[i]], [], [("dma_start", dict(out=dst_ap, in_=sbv))])

    def cv_run():
        nj = len(cv_jobs)
        for n in range(min(NSTG - 1, nj)):
            cv_load(n)
        for n in range(nj):
            if n + NSTG - 1 < nj:
                cv_load(n + NSTG - 1)
            cv_cast_store(n)

    def conv2d(src, dst, R, C, l):
        if C <= 1024:
            a = 2048 // C
            nchunk = R // (128 * a)
            sv = src.rearrange("(n p a) c -> n p a c", p=128, a=a)
            dv = dst.rearrange("(n p a) c -> n p a c", p=128, a=a)
            for n in range(nchunk):
                conv_chunk(sv[n], dv[n], (a, C), l)
        else:
            for r0 in range(0, R, 128):
                for c0 in range(0, C, 2048):
                    c1 = min(c0 + 2048, C)
                    conv_chunk(src[r0:r0 + 128, c0:c1], dst[r0:r0 + 128, c0:c1], (c1 - c0,), l)

    for l in range(NL):
        w = W[l]
        conv2d(w["w_in"], w["winb"], D, 3072, l)
        conv2d(w["w_out"], w["woutb"], D, D, l)
        conv2d(w["w_q"], w["wqb"], D, 2048, l)
        conv2d(w["keysT"], w["keysb"], 2048, 128, l)
        conv2d(w["u"], w["uvb"][:, 0:1024], NE, D, l)
        conv2d(w["v"], w["uvb"][:, 1024:2048], NE, D, l)
    cv_run()

    thr = _bucket_thresholds()
    acc_t = A.alloc([1, 256], F32)[:, 0, :]
    tmp_t = A.alloc([1, 256], F32)[:, 0, :]
    msk_t = A.alloc([1, 256], F32)[:, 0, :]
    lam_s = A.alloc([4, 64], F32)
    lam_j = A.alloc([1, 64], F32)[:, 0, :]
    b_t = Buf(); b_M = Buf(); b_lam = Buf()
    s_lam = P.dsem(); s_cw = P.dsem(); s_ky = P.dsem()
    P.op("vector", [b_const], [b_t], "tensor_tensor", out=dl[:], in0=rb[:, 4:128], in1=rb[:, 0:124], op=ALU.subtract)
    P.op("vector", [b_const], [b_t], "tensor_scalar", out=negc[:], in0=rb[:, 124:128], scalar1=-1.0, scalar2=None, op0=ALU.mult)
    P.op("vector", [b_const], [b_t], "tensor_scalar", out=msk_t, in0=dist[:], scalar1=0.0, scalar2=None, op0=ALU.is_ge)
    for hh in range(4):
        P.op("vector", [b_const], [b_t], "tensor_scalar", out=acc_t, in0=dist[:], scalar1=0.0, scalar2=rb[:, hh:hh + 1],
             op0=ALU.mult, op1=ALU.add)
        for b in range(1, 32):
            P.op("vector", [b_const, b_t], [b_t], "tensor_scalar", out=tmp_t, in0=dist[:], scalar1=float(thr[b - 1]),
                 scalar2=dl[:, (b - 1) * 4 + hh:(b - 1) * 4 + hh + 1], op0=ALU.is_ge, op1=ALU.mult)
            P.op("vector", [b_t], [b_t], "tensor_tensor", out=acc_t, in0=acc_t, in1=tmp_t, op=ALU.add)
        P.op("scalar", [b_t], [b_t], "activation", out=tmp_t, in_=acc_t, func=AF.Exp, bias=negc[:, hh:hh + 1], scale=1.0)
        P.op("vector", [b_t], [b_M, b_t], "tensor_tensor", out=Mt[:, hh, :], in0=tmp_t, in1=msk_t, op=ALU.mult)

    for l in range(NL):
        w = W[l]
        P.dma("sync", s_lam, [b_lam], [b_lam], [
            ("dma_start", dict(out=lam_s[:, 0, :], in_=w["lq1"].partition_broadcast(128))),
            ("dma_start", dict(out=lam_s[:, 1, :], in_=w["lk1"].partition_broadcast(128))),
            ("dma_start", dict(out=lam_s[:, 2, :], in_=w["lq2"].partition_broadcast(128))),
            ("dma_start", dict(out=lam_s[:, 3, :], in_=w["lk2"].partition_broadcast(128))),
            ("dma_start", dict(out=gsc[:, l:l + 1], in_=w["subg"].rearrange("(p o) -> p o", o=1)))])
        c0 = 4 * l
        P.op("vector", [b_lam], [b_lam], "scalar_tensor_tensor", out=lam_j, in0=lam_s[:, 0, :], scalar=1.0, in1=lam_s[:, 1, :],
             op0=ALU.mult, op1=ALU.mult, accum_out=lamt[:, c0:c0 + 1])
        P.op("vector", [b_lam], [b_lam], "scalar_tensor_tensor", out=lam_j, in0=lam_s[:, 2, :], scalar=1.0, in1=lam_s[:, 3, :],
             op0=ALU.mult, op1=ALU.mult, accum_out=lamt[:, c0 + 1:c0 + 2])
        P.op("scalar", [b_lam], [b_lam], "activation", out=lamt[:, c0 + 2:c0 + 4], in_=lamt[:, c0:c0 + 2], func=AF.Exp)
        P.op("vector", [b_lam, b_const], [b_lam], "tensor_tensor", out=lamt[:, c0:c0 + 1], in0=lamt[:, c0 + 3:c0 + 4],
             in1=lamt[:, c0 + 2:c0 + 3], op=ALU.subtract)
        P.op("vector", [b_lam, b_const], [b_lam], "tensor_tensor", out=neglam[:, l:l + 1], in0=lamt[:, c0:c0 + 1],
             in1=lamt[:, 4 * NL + l:4 * NL + l + 1], op=ALU.subtract)
        P.op("vector", [b_lam, b_const], [b_lam], "tensor_scalar", out=lamt[:, c0 + 1:c0 + 2],
             in0=lamt[:, 4 * NL + l:4 * NL + l + 1], scalar1=-1.0, scalar2=1.0, op0=ALU.mult, op1=ALU.add)
        P.op("vector", [b_lam], [b_lam], "tensor_tensor", out=gsc[:, l:l + 1], in0=gsc[:, l:l + 1], in1=lamt[:, c0 + 1:c0 + 2],
             op=ALU.mult)
    for e_ in ENGS:
        for i_ in range(NSTG):
            P._wait(e_, (s_out[i_].sem, s_out[i_].cnt))
    P.barrier()

    def layernorm(g_ap, b_ap):
        gi, bi = 0, 1
        P.dma("sync", s_lnp, [b_lnp], [b_lnp], [("dma_start", dict(out=LNP[:, 0, :], in_=g_ap.partition_broadcast(128))),
                                               ("dma_start", dict(out=LNP[:, 1, :], in_=b_ap.partition_broadcast(128)))])
        for t in range(NT):
            P.opm("vector", [b_h[t]], [b_ln], [
                ("bn_stats", dict(out=lnst[:, t, 0:6], in_=h[:, t, 0:512])),
                ("bn_stats", dict(out=lnst[:, t, 6:12], in_=h[:, t, 512:1024]))])
            P.op("vector", [b_ln], [b_ln], "bn_aggr", out=lnmv[:, t, :], in_=lnst[:, t, :])
        P.op("vector", [b_ln], [b_ln], "tensor_scalar", out=lnve[:], in0=lnmv[:, :, 1], scalar1=LN_EPS, scalar2=None, op0=ALU.add)
        P.op("scalar", [b_ln], [b_ln], "activation", out=lnve[:], in_=lnve[:], func=AF.Sqrt)
        P.op("vector", [b_ln], [b_ln], "reciprocal", out=lnrs[:], in_=lnve[:])
        for t in range(NT):
            P.op("vector", [b_ln, b_h[t]], [b_h[t]], "tensor_scalar", out=h[:, t, :], in0=h[:, t, :], scalar1=lnmv[:, t, 0:1],
                 scalar2=lnrs[:, t:t + 1], op0=ALU.subtract, op1=ALU.mult)
            P.op("gpsimd", [b_lnp, b_h[t]], [b_h[t]], "tensor_tensor", out=h[:, t, :], in0=h[:, t, :], in1=LNP[:, gi, :], op=ALU.mult)
            P.op("gpsimd", [b_lnp, b_h[t]], [b_h[t]], "tensor_tensor", out=h[:, t, :], in0=h[:, t, :], in1=LNP[:, bi, :], op=ALU.add)

    evac_n = [0]

    def evac(reads, writes, out, in_, scale=None):
        evac_n[0] += 1
        if scale is not None or evac_n[0] % 2:
            kw = dict(out=out, in_=in_, func=AF.Copy)
            if scale is not None:
                kw["scale"] = scale
            P.op("scalar", reads, writes, "activation", **kw)
        else:
            P.op("vector", reads, writes, "tensor_copy", out=out, in_=in_)

    def build_hT(scale_alpha):
        for t in range(NT):
            bk = (t % 2) * 2
            pT = ps[:, bk:bk + 2, :].rearrange("p a (c n) -> p (a c) n", n=128)
            P.opm("tensor", [b_h[t], b_const], [b_ps[bk], b_ps[bk + 1]], [
                ("transpose", dict(out=pT[:, c, :], in_=h[:, t, c * 128:(c + 1) * 128], identity=identF[:])) for c in range(8)])
            evac([b_ps[bk], b_ps[bk + 1]], [b_hT[t]], hT[:, :, t * 128:(t + 1) * 128], pT)
            if scale_alpha:
                P.op("gpsimd", [b_h[t]], [b_h[t]], "tensor_scalar", out=h[:, t, :], in0=h[:, t, :], scalar1=ALPHA, scalar2=None, op0=ALU.mult)

    s_w = [P.dsem() for _ in range(4)]
    s_ring = [P.dsem() for _ in range(RING)]

    s_h = P.dsem(); s_st = P.dsem()
    P.start_body()
    for s in range(1):
        def hin(t0, t1):
            return lambda sv: h_in[sv].rearrange("(t p) d -> p t d", p=128)[:, t0:t1, :]

        def hout(t0, t1):
            return lambda sv: h_out[sv].rearrange("(t p) d -> p t d", p=128)[:, t0:t1, :]
        P.dma("sync", s_h, [], b_h, [("dma_start", dict(out=h[:, t0:t0 + 4, :], in_=hin(t0, t0 + 4))) for t0 in (0, 4, 8, 12)]
              + [("dma_start", dict(out=h[:, 16:17, :], in_=hin(16, 17)))])
        if do_ln_in:
            layernorm(lnin_g, lnin_b)
        for l in range(NL):
            w = W[l]
            winv = w["winb"].rearrange("(c p) n -> p c n", p=128)
            wov = w["woutb"].rearrange("(c p) n -> p c n", p=128)
            wqv = w["wqb"].rearrange("(c p) n -> p c n", p=128)

            build_hT(True)
            P.barrier()
            A.reset()
            wsl = [A.alloc([8, 3, 128], BF16) for _ in range(2)]
            wo = [A.alloc([1, 1024], BF16)[:, 0, :] for _ in range(2)]
            b_wsl = [Buf(), Buf()]
            XT = A.alloc([1, LP], BF16)[:, 0, :]
            b_XT = Buf()
            qT = A.alloc([1, LP], BF16)[:, 0, :]
            kT = A.alloc([1, LP], BF16)[:, 0, :]
            Vh = A.alloc([NT, 130], BF16)
            b_q = Buf(); b_k = Buf(); b_v = Buf()
            Eb = [A.alloc([2, QC * 128], BF16) for _ in range(2)]
            b_E = [Buf(), Buf()]
            Gc = A.alloc([1, LP + 2], F32)[:, 0, :]
            zs = A.alloc([1, 512], F32)[:, 0, :]
            yt = A.alloc([1, 512], F32)[:, 0, :]
            cw = A.alloc([4, 3], F32)
            Oh = A.alloc([NT, 128], F32)
            ssq = A.alloc([1, NT], F32)[:, 0, :]
            rr = A.alloc([1, 4], F32)[:, 0, :]
            t1 = A.alloc([1, 128], F32)[:, 0, :]
            jk = A.alloc([1, 128], F32)[:, 0, :]
            onb = A.alloc([1, 128], BF16)[:, 0, :]
            b_G = Buf(); b_zs = Buf(); b_y = Buf(); b_cw = Buf(); b_O = Buf(); b_sm = Buf()

            P.dma("sync", s_cw, [b_cw], [b_cw], [("dma_start", dict(out=cw, in_=w["convw"].rearrange("(c p) j -> p c j", p=128)))])
            P.op("vector", [], [b_v], "memset", ap=Vh[:, :, 128:129], constant=1.0)
            P.op("vector", [], [b_G], "memset", ap=Gc[:, 0:2], constant=0.0)

            def load_w(j, cols, worow):
                i = j % 2
                P.dma("sync", s_w[i], [b_wconv[l]], [b_wsl[i]], [
                    ("dma_start", dict(out=wsl[i][:, :, k, :], in_=winv[:, :, cols[k]:cols[k] + 128])) for k in range(3)]
                    + [("dma_start", dict(out=wo[i], in_=wov[:, worow, :]))])

            def contribute(j):
                i = j % 2
                for t in range(NT):
                    cb = 2 * (t % 2)
                    P.opm("tensor", [b_XT, b_wsl[i]], [b_ps[cb], b_ps[cb + 1]], [
                        ("matmul", dict(out=ps[:, cb, :], lhsT=XT[:, t * 128:(t + 1) * 128], rhs=wo[i][:, 0:512], start=True, stop=True)),
                        ("matmul", dict(out=ps[:, cb + 1, :], lhsT=XT[:, t * 128:(t + 1) * 128], rhs=wo[i][:, 512:1024], start=True, stop=True))])
                    P.op("vector", [b_ps[cb], b_ps[cb + 1], b_h[t]], [b_h[t]], "tensor_tensor", out=h[:, t, :], in0=h[:, t, :],
                         in1=ps[:, cb:cb + 2, :].rearrange("p a n -> p (a n)"), op=ALU.add)

            tchunks = [(c0, min(c0 + 512, LP)) for c0 in range(0, LP, 512)]
            jobs = [("conv", c) for c in range(4)] + [("attn", hh) for hh in range(4)]

            def job_cols(job):
                kind, c = job
                if kind == "conv":
                    return [1536 + c * 128, 2048 + c * 128, 2560 + c * 128], 4 + c
                return [c * 128, 512 + c * 128, 1024 + c * 128], c

            load_w(0, *job_cols(jobs[0]))
            for j, job in enumerate(jobs):
                i = j % 2
                if j + 1 < len(jobs):
                    load_w(j + 1, *job_cols(jobs[j + 1]))
                kind, c = job
                if kind == "conv":
                    for (c0, c1) in tchunks:
                        n = c1 - c0
                        tl = list(range(c0 // 128, c1 // 128))
                        for k in range(3):
                            P.opm("tensor", [b_hT[t] for t in tl] + [b_wsl[i]], [b_ps[2 + k]], [
                                ("matmul", dict(out=ps[:, 2 + k, 0:n], lhsT=wsl[i][:, dc, k, :], rhs=hT[:, dc, c0:c1],
                                                start=(dc == 0), stop=(dc == 7))) for dc in range(8)])
                        P.op("scalar", [b_ps[4]], [b_zs], "activation", out=zs[:, 0:n], in_=ps[:, 4, 0:n], func=AF.Copy)
                        P.op("vector", [b_ps[3], b_zs], [b_G], "tensor_tensor", out=Gc[:, 2 + c0:2 + c1], in0=ps[:, 3, 0:n], in1=zs[:, 0:n], op=ALU.mult)
                        P.op("vector", [b_G, b_cw], [b_y], "tensor_scalar", out=yt[:, 0:n], in0=Gc[:, c0:c1], scalar1=cw[:, c, 0:1], scalar2=None, op0=ALU.mult)
                        P.op("vector", [b_G, b_cw, b_y], [b_y], "scalar_tensor_tensor", out=yt[:, 0:n], in0=Gc[:, c0 + 1:c1 + 1], scalar=cw[:, c, 1:2],
                             in1=yt[:, 0:n], op0=ALU.mult, op1=ALU.add)
                        P.op("vector", [b_G, b_cw, b_y], [b_y], "scalar_tensor_tensor", out=yt[:, 0:n], in0=Gc[:, c0 + 2:c1 + 2], scalar=cw[:, c, 2:3],
                             in1=yt[:, 0:n], op0=ALU.mult, op1=ALU.add)
                        P.op("vector", [b_y, b_ps[2]], [b_XT], "tensor_tensor", out=XT[:, c0:c1], in0=yt[:, 0:n], in1=ps[:, 2, 0:n], op=ALU.mult)
                    contribute(j)
                    continue
                hh = c
                for (c0, c1) in tchunks:
                    n = c1 - c0
                    tl = list(range(c0 // 128, c1 // 128))
                    for k, (dst, bd) in enumerate(((qT, b_q), (kT, b_k))):
                        P.opm("tensor", [b_hT[t] for t in tl] + [b_wsl[i]], [b_ps[k]], [
                            ("matmul", dict(out=ps[:, k, 0:n], lhsT=wsl[i][:, dc, k, :], rhs=hT[:, dc, c0:c1],
                                            start=(dc == 0), stop=(dc == 7))) for dc in range(8)])
                        evac([b_ps[k]], [bd], dst[:, c0:c1], ps[:, k, 0:n], scale=(0.125 if k == 0 else None))
                for t in range(NT):
                    bk = t % 2
                    P.opm("tensor", [b_hT[t], b_wsl[i]], [b_ps[bk]], [
                        ("matmul", dict(out=ps[:, bk, 0:128], lhsT=hT[:, dc, t * 128:(t + 1) * 128], rhs=wsl[i][:, dc, 2, :],
                                        start=(dc == 0), stop=(dc == 7))) for dc in range(8)])
                    evac([b_ps[bk]], [b_v], Vh[:, t, 0:128], ps[:, bk, 0:128])
                steps = []
                for ci, qa in enumerate(range(0, NT, QC)):
                    qb = min(qa + QC, NT)
                    for kj in range(0, qb):
                        steps.append((ci, qa, qb, kj))

                def part_a(n):
                    ci, qa, qb, kj = steps[n]
                    e = n % 2
                    q_lo = max(qa, kj)
                    col0 = (q_lo - qa) * 128
                    col1 = (qb - qa) * 128
                    sb = 2 + 2 * e
                    for m in range(2):
                        P.op("tensor", [b_q, b_k], [b_ps[sb + m]], "matmul", out=ps[:, sb + m, col0:col1],
                             lhsT=kT[m * 64:(m + 1) * 64, kj * 128:(kj + 1) * 128], rhs=qT[m * 64:(m + 1) * 64, qa * 128 + col0:qa * 128 + col1],
                             start=True, stop=True)
                    for m in range(2):
                        P.op("scalar", [b_ps[sb + m], b_const], [b_E[e]], "activation", out=Eb[e][:, m, col0:col1], in_=ps[:, sb + m, col0:col1],
                             func=AF.Exp, bias=rb[:, 124 + hh:125 + hh], scale=1.0)
                    for qi in range(q_lo, qb):
                        off = qi - kj
                        if off > 1:
                            continue
                        ql = qi - qa
                        P.op("vector", [b_E[e], b_M], [b_E[e]], "tensor_tensor", out=Eb[e][:, :, ql * 128:(ql + 1) * 128],
                             in0=Eb[e][:, :, ql * 128:(ql + 1) * 128],
                             in1=Mt[:, hh, off * 128:(off + 1) * 128].unsqueeze(1).to_broadcast([128, 2, 128]), op=ALU.mult)

                def part_b(n):
                    ci, qa, qb, kj = steps[n]
                    e = n % 2
                    ab = 6 if ci % 2 == 0 else 0
                    q_lo = max(qa, kj)
                    insts = []
                    for qi in range(q_lo, qb):
                        ql = qi - qa
                        for m in range(2):
                            insts.append(("matmul", dict(out=ps[:, ab + m, ql * 129:(ql + 1) * 129], lhsT=Eb[e][:, m, ql * 128:(ql + 1) * 128],
                                                         rhs=Vh[:, kj, 0:129], start=(kj == 0 and ql == 0), stop=(kj == qi),
                                                         skip_group_check=True)))
                    P.opm("tensor", [b_E[e], b_v], [b_ps[ab], b_ps[ab + 1]], insts)
                    if kj != qb - 1:
                        return
                    for qi in range(qa, qb):
                        ql = qi - qa
                        for m in range(2):
                            P.op("vector", [b_ps[ab + m]], [b_sm], "reciprocal", out=rr[:, m:m + 1], in_=ps[:, ab + m, ql * 129 + 128:ql * 129 + 129])
                        P.op("vector", [b_sm, b_lam], [b_sm], "tensor_tensor", out=rr[:, 2:3], in0=rr[:, 1:2], in1=neglam[:, l:l + 1], op=ALU.mult)
                        P.op("vector", [b_sm, b_ps[ab + 1]], [b_sm], "tensor_scalar", out=t1, in0=ps[:, ab + 1, ql * 129:ql * 129 + 128], scalar1=rr[:, 2:3],
                             scalar2=None, op0=ALU.mult)
                        P.op("vector", [b_sm, b_ps[ab]], [b_O], "scalar_tensor_tensor", out=Oh[:, qi, :], in0=ps[:, ab, ql * 129:ql * 129 + 128],
                             scalar=rr[:, 0:1], in1=t1, op0=ALU.mult, op1=ALU.add)
                        P.op("vector", [b_O], [b_sm], "scalar_tensor_tensor", out=jk, in0=Oh[:, qi, :], scalar=1.0, in1=Oh[:, qi, :],
                             op0=ALU.mult, op1=ALU.mult, accum_out=ssq[:, qi:qi + 1])

                part_a(0)
                for n in range(len(steps)):
                    if n + 1 < len(steps):
                        part_a(n + 1)
                    part_b(n)
                P.op("vector", [b_sm], [b_sm], "tensor_scalar", out=ssq, in0=ssq, scalar1=1.0 / 128.0, scalar2=RMS_EPS, op0=ALU.mult, op1=ALU.add)
                P.op("scalar", [b_sm], [b_sm], "activation", out=ssq, in_=ssq, func=AF.Sqrt)
                P.op("vector", [b_sm], [b_sm], "reciprocal", out=ssq, in_=ssq)
                for t in range(NT):
                    P.op("vector", [b_O, b_sm], [b_sm], "tensor_scalar", out=onb, in0=Oh[:, t, :], scalar1=ssq[:, t:t + 1], scalar2=None, op0=ALU.mult)
                    bk = t % 2
                    pTb = ps[:, bk, 0:64].bitcast(BF16)
                    P.op("tensor", [b_sm, b_const], [b_ps[bk]], "transpose", out=pTb, in_=onb, identity=identB[:])
                    P.op("scalar", [b_ps[bk], b_lam], [b_XT], "activation", out=XT[:, t * 128:(t + 1) * 128], in_=pTb, func=AF.Copy, scale=gsc[:, l:l + 1])
                contribute(j)

            layernorm(w["ln1g"], w["ln1b"])
            build_hT(False)
            P.barrier()

            A.reset()
            IDX = A.alloc([1, LP], I32)[:, 0, :]
            GT = A.alloc([1, LP], F32)[:, 0, :]
            kyb = A.alloc([16, 128], BF16)
            wqs = [A.alloc([8, 128], BF16) for _ in range(2)]
            qTp = A.alloc([16, 256], BF16)
            S = A.alloc([16, 128], F32)
            SCR = A.alloc([16, 128], F32)
            OH = SCR[:].rearrange("p g n -> p (g n)").rearrange("p (h k i) -> p h k i", h=8, k=16)
            V1 = A.alloc([16, 16], F32)
            I1 = A.alloc([16, 16], U32)
            I1f = A.alloc([16, 16], F32)
            CAND = A.alloc([8, 256], F32)
            CSCR = A.alloc([8, 256], F32)
            SC = A.alloc([8, 16], F32)
            CI = A.alloc([8, 16], U32)
            CIh = A.alloc([8, 16], U32)
            CIl = A.alloc([8, 16], U32)
            CIhf = A.alloc([8, 16], F32)
            CIlf = A.alloc([8, 16], F32)
            E1 = A.alloc([8, 16], F32)
            E2 = A.alloc([8, 16], F32)
            EIDX = A.alloc([8, 16], F32)
            G16 = A.alloc([8, 16], F32)
            sm8 = A.alloc([8], F32)
            d1_end = A.off
            b_ky = Buf(); b_wq = [Buf(), Buf()]; b_qp = Buf(); b_S = Buf(); b_tk = Buf(); b_idx = Buf(); b_gt = Buf()
            P.dma("sync", s_ky, [b_ky], [b_ky], [("dma_start", dict(out=kyb, in_=w["keysb"].rearrange("(g d) n -> d g n", d=128)))])
            nwq = 0
            for c0 in range(0, LP, 256):
                c1 = min(c0 + 256, LP)
                n = c1 - c0
                tl = list(range(c0 // 128, c1 // 128))
                for g in range(16):
                    i = nwq % 2
                    nwq += 1
                    P.dma("sync", s_w[2 + i], [b_wconv[l]], [b_wq[i]], [("dma_start", dict(out=wqs[i], in_=wqv[:, :, g * 128:(g + 1) * 128]))])
                    bk = g % 2
                    P.opm("tensor", [b_hT[t] for t in tl] + [b_wq[i]], [b_ps[bk]], [
                        ("matmul", dict(out=ps[:, bk, 0:n], lhsT=wqs[i][:, dc, :], rhs=hT[:, dc, c0:c1], start=(dc == 0), stop=(dc == 7)))
                        for dc in range(8)])
                    evac([b_ps[bk]], [b_qp], qTp[:, g, 0:n], ps[:, bk, 0:n])
                for t in tl:
                    tc0 = t * 128 - c0
                    P.opm("tensor", [b_qp, b_ky], [b_ps[4], b_ps[5], b_ps[6], b_ps[7]], [
                        ("matmul", dict(out=ps[:, 4 + g // 4, (g % 4) * 128:(g % 4 + 1) * 128], lhsT=qTp[:, g, tc0:tc0 + 128], rhs=kyb[:, g, :],
                                        start=True, stop=True, skip_group_check=True)) for g in range(16)])
                    P.op("scalar", [b_ps[4], b_ps[5], b_ps[6], b_ps[7]], [b_S], "activation", out=S[:].rearrange("p g n -> p (g n)"),
                         in_=ps[:, 4:8, :].rearrange("p a n -> p (a n)"), func=AF.Copy)
                    bg = [[Buf() for _ in range(16)] for _ in range(4)]
                    for g in range(16):
                        P.op("vector", [b_S], [bg[0][g]], "max", out=V1[:, g, 0:8], in_=S[:, g, :])
                    for g in range(16):
                        P.op("vector", [b_S, bg[0][g]], [bg[1][g]], "match_replace", out=SCR[:, g, :], in_to_replace=V1[:, g, 0:8], in_values=S[:, g, :], imm_value=-1e30)
                    for g in range(16):
                        P.op("vector", [bg[1][g]], [bg[2][g]], "max", out=V1[:, g, 8:16], in_=SCR[:, g, :])
                    for g in range(16):
                        P.op("vector", [b_S, bg[0][g]], [bg[3][g]], "max_index", out=I1[:, g, 0:8], in_max=V1[:, g, 0:8], in_values=S[:, g, :])
                    for g in range(16):
                        P.op("vector", [bg[1][g], bg[2][g]], [bg[3][g]], "max_index", out=I1[:, g, 8:16], in_max=V1[:, g, 8:16], in_values=SCR[:, g, :])
                    allg = [x for row in bg for x in row]
                    P.op("vector", allg + [b_tk], [b_tk], "tensor_copy", out=I1f[:], in_=I1[:])
                    I1v = I1f[:].rearrange("p (h two) k -> p h two k", two=2)
                    V1v = V1[:].rearrange("p (h two) k -> p h two k", two=2)
                    P.op("vector", [b_tk], [b_tk], "tensor_scalar", out=I1v[:, :, 0, :], in0=I1v[:, :, 0, :], scalar1=128.0, scalar2=None, op0=ALU.mult)
                    C4 = CAND[:].rearrange("p h (i j) -> p h i j", j=16)
                    P.op("vector", allg + [b_tk], [b_tk], "tensor_tensor", out=C4, in0=V1v[:, :, 0, :].unsqueeze(3).to_broadcast([128, 8, 16, 16]),
                         in1=V1v[:, :, 1, :].unsqueeze(2).to_broadcast([128, 8, 16, 16]), op=ALU.add)
                    bh = [[Buf() for _ in range(8)] for _ in range(4)]
                    for hd in range(8):
                        P.op("vector", [b_tk], [bh[0][hd]], "max", out=SC[:, hd, 0:8], in_=CAND[:, hd, :])
                    for hd in range(8):
                        P.op("vector", [b_tk, bh[0][hd]], [bh[1][hd]], "match_replace", out=CSCR[:, hd, :], in_to_replace=SC[:, hd, 0:8], in_values=CAND[:, hd, :], imm_value=-1e30)
                    for hd in range(8):
                        P.op("vector", [bh[1][hd]], [bh[2][hd]], "max", out=SC[:, hd, 8:16], in_=CSCR[:, hd, :])
                    for hd in range(8):
                        P.op("vector", [b_tk, bh[0][hd]], [bh[3][hd]], "max_index", out=CI[:, hd, 0:8], in_max=SC[:, hd, 0:8], in_values=CAND[:, hd, :])
                    for hd in range(8):
                        P.op("vector", [bh[1][hd], bh[2][hd]], [bh[3][hd]], "max_index", out=CI[:, hd, 8:16], in_max=SC[:, hd, 8:16], in_values=CSCR[:, hd, :])
                    allh = [x for row in bh for x in row]
                    P.op("vector", allh + [b_tk], [b_tk], "tensor_scalar", out=CIh[:], in0=CI[:], scalar1=4, scalar2=None, op0=ALU.logical_shift_right)
                    P.op("vector", [b_tk], [b_tk], "tensor_scalar", out=CIl[:], in0=CI[:], scalar1=15, scalar2=None, op0=ALU.bitwise_and)
                    P.op("vector", [b_tk], [b_tk], "tensor_copy", out=CIhf[:], in_=CIh[:])
                    P.op("vector", [b_tk], [b_tk], "tensor_copy", out=CIlf[:], in_=CIl[:])
                    io4 = iota16[:].unsqueeze(1).unsqueeze(1).to_broadcast([128, 8, 16, 16])
                    for (cif, two, eo) in ((CIhf, 0, E1), (CIlf, 1, E2)):
                        P.op("vector", [b_tk, b_const], [b_tk], "tensor_tensor", out=OH, in0=cif[:].unsqueeze(3).to_broadcast([128, 8, 16, 16]), in1=io4, op=ALU.is_equal)
                        P.op("vector", [b_tk], [b_tk], "tensor_tensor", out=OH, in0=OH, in1=I1v[:, :, two, :].unsqueeze(2).to_broadcast([128, 8, 16, 16]), op=ALU.mult)
                        P.op("vector", [b_tk], [b_tk], "tensor_reduce", out=eo[:], in_=OH, axis=AX.X, op=ALU.add)
                    P.op("vector", [b_tk], [b_tk], "tensor_tensor", out=EIDX[:], in0=E1[:], in1=E2[:], op=ALU.add)
                    P.op("vector", [b_tk], [b_tk], "tensor_scalar", out=EIDX[:], in0=EIDX[:], scalar1=float(NE - 1), scalar2=None, op0=ALU.min)
                    P.op("vector", [b_tk], [b_tk], "tensor_tensor", out=G16[:], in0=SC[:], in1=SC[:, :, 0:1].to_broadcast([128, 8, 16]), op=ALU.subtract)
                    P.op("scalar", [b_tk], [b_tk], "activation", out=G16[:], in_=G16[:], func=AF.Exp)
                    P.op("vector", [b_tk], [b_tk], "tensor_reduce", out=sm8, in_=G16[:], axis=AX.X, op=ALU.add)
                    P.op("vector", [b_tk], [b_tk], "reciprocal", out=sm8, in_=sm8)
                    P.op("vector", [b_tk], [b_tk], "tensor_tensor", out=G16[:], in0=G16[:], in1=sm8.unsqueeze(2).to_broadcast([128, 8, 16]), op=ALU.mult)
                    P.opm("tensor", [b_tk, b_const], [b_ps[2], b_ps[3]], [
                        ("transpose", dict(out=ps[:, 2, 0:128], in_=EIDX[:].rearrange("p h k -> p (h k)"), identity=identF[:])),
                        ("transpose", dict(out=ps[:, 3, 0:128], in_=G16[:].rearrange("p h k -> p (h k)"), identity=identF[:]))])
                    P.op("vector", [b_ps[2]], [b_idx], "tensor_copy", out=IDX[:, t * 128:(t + 1) * 128], in_=ps[:, 2, 0:128])
                    P.op("scalar", [b_ps[3]], [b_gt], "activation", out=GT[:, t * 128:(t + 1) * 128], in_=ps[:, 3, 0:128], func=AF.Copy)
            P.barrier()

            A.off = (2 * LP * 4 + 63) // 64 * 64
            hb = [A.alloc([D], BF16) for _ in range(2)]
            a0 = A.alloc([2, 128], F32)
            gel = A.alloc([2, 128], F32)
            Wg = A.alloc([2, 128], F32)
            junk = A.alloc([D], BF16)
            Lt = A.alloc([4, 128], BF16)
            b_hb = [Buf(), Buf()]; b_ring = [Buf() for _ in range(RING)]
            b_a0 = [Buf() for _ in range(4)]; b_gl = [Buf() for _ in range(4)]; b_Lt = [Buf() for _ in range(4)]; b_W = [Buf() for _ in range(4)]
            toks = [(t, p) for t in range(NT) for p in range(128 if t < 16 else 16)]
            NTOK = len(toks)
            KSK = 2

            def gather(gi):
                t, p = toks[gi]
                sl = gi % RING
                P.dma("gpsimd", s_ring[sl], [b_idx], [b_ring[sl]], [("indirect_dma_start", dict(
                    out=ring[:, sl, :], out_offset=None, in_=w["uvb"],
                    in_offset=bass.IndirectOffsetOnAxis(ap=IDX[:, t * 128 + p:t * 128 + p + 1], axis=0),
                    bounds_check="NEREG", oob_is_err=False))])

            for gi in range(min(RING, NTOK)):
                gather(gi)
            for i in range(NTOK + KSK):
                j = i - KSK
                if j >= 0:
                    t, p = toks[j]
                    P.op("vector", [b_gl[j % 4], b_gt, b_const], [b_Lt[j % 4]], "tensor_scalar", out=Lt[:, j % 4, :],
                         in0=zmask[:, 127 - p:255 - p], scalar1=gel[:, t % 2, p:p + 1], scalar2=GT[:, t * 128 + p:t * 128 + p + 1],
                         op0=ALU.mult, op1=ALU.mult)
                if i < NTOK:
                    t, p = toks[i]
                    hi = t % 2
                    sl = i % RING
                    kb = (i % 2) * 2
                    if p == 0:
                        P.op("scalar", [b_h[t]], [b_hb[hi]], "activation", out=hb[hi], in_=h[:, t, :], func=AF.Copy)
                    P.opm("tensor", [b_hb[hi], b_const], [b_ps[kb], b_ps[kb + 1]], [
                        ("matmul", dict(out=ps[:, kb + hf, :], lhsT=identB[:, p:p + 1].to_broadcast([128, 128]),
                                        rhs=hb[hi][:, hf * 512:(hf + 1) * 512], start=True, stop=True)) for hf in range(2)])
                    P.op("vector", [b_ps[kb], b_ps[kb + 1], b_ring[sl]], [b_a0[i % 4]], "scalar_tensor_tensor", out=junk,
                         in0=ring[:, sl, 0:1024], scalar=1.0, in1=ps[:, kb:kb + 2, :].rearrange("p a n -> p (a n)"),
                         op0=ALU.mult, op1=ALU.mult, accum_out=a0[:, hi, p:p + 1])
                    P.op("scalar", [b_a0[i % 4]], [b_gl[i % 4]], "activation", out=gel[:, hi, p:p + 1], in_=a0[:, hi, p:p + 1], func=AF.Gelu)
                if j >= 0:
                    t, p = toks[j]
                    npt = 128 if t < 16 else 16
                    sl = j % RING
                    accb = 4 + 2 * (t % 2)
                    P.opm("tensor", [b_Lt[j % 4], b_ring[sl]], [b_ps[accb], b_ps[accb + 1]], [
                        ("matmul", dict(out=ps[:, accb, :], lhsT=Lt[:, j % 4, :], rhs=ring[:, sl, 1024:1536], start=(p == 0), stop=(p == npt - 1))),
                        ("matmul", dict(out=ps[:, accb + 1, :], lhsT=Lt[:, j % 4, :], rhs=ring[:, sl, 1536:2048], start=(p == 0), stop=(p == npt - 1)))])
                    if p == npt - 1:
                        P.op("vector", [b_ps[accb], b_ps[accb + 1], b_h[t]], [b_h[t]], "scalar_tensor_tensor", out=h[:, t, :], in0=h[:, t, :],
                             scalar=ALPHA, in1=ps[:, accb:accb + 2, :].rearrange("p a n -> p (a n)"), op0=ALU.mult, op1=ALU.add)
                    if j + RING < NTOK:
                        gather(j + RING)
            P.barrier()
            layernorm(w["ln2g"], w["ln2b"])
        P.dma("sync", s_st, b_h, [], [("dma_start", dict(out=hout(t0, t0 + 4), in_=h[:, t0:t0 + 4, :])) for t0 in (0, 4, 8, 12)]
              + [("dma_start", dict(out=hout(16, 17), in_=h[:, 16:17, :]))])
    P._wait("sync", (s_st.sem, s_st.cnt))
    P.iter_sync(None)
    P.emit(NSEQ)
    P.stack.close()
    return nc


_CACHE = {}


def _get_nc(NL, do_ln_in):
    key = (NL, do_ln_in)
    if key not in _CACHE:
        _CACHE[key] = build(NL, do_ln_in)
    return _CACHE[key]


def kernel(x, meta_tokens, ln_in_g, ln_in_b, rel_bias, w_in, conv_w, lambda_q1, lambda_k1, lambda_q2, lambda_k2,
           subln_g, w_out, ln1_g, ln1_b, peer_w_q, peer_sub_keys, peer_u, peer_v, ln2_g, ln2_b):
    f = lambda a: np.ascontiguousarray(np.asarray(a, dtype=np.float32))
    x = f(x)
    B = x.shape[0]
    ncore = 8
    h0 = np.zeros((B, LP, D), np.float32)
    h0[:, :NMETA] = f(meta_tokens)[None]
    h0[:, NMETA:L] = x
    identF = np.eye(128, dtype=np.float32)
    identB = np.eye(128).astype(ml_dtypes.bfloat16)
    zmask = np.zeros((128, 255), np.float32); zmask[:, 127] = 1.0
    dist = (np.arange(256)[None, :] - np.arange(128)[:, None]).astype(np.float32)
    keysT = np.ascontiguousarray(np.transpose(f(peer_sub_keys), (0, 1, 2, 4, 3))).reshape(DEPTH, 2048, 128)
    convwT = np.ascontiguousarray(np.transpose(f(conv_w), (0, 2, 1)))
    lam_init = np.array([0.8 - 0.6 * math.exp(-0.3 * l) for l in range(DEPTH)], np.float32)

    def layer_inputs(l, j):
        return {f"w_in{j}": f(w_in[l]), f"w_out{j}": f(w_out[l]), f"w_q{j}": f(peer_w_q[l]), f"keysT{j}": keysT[l],
                f"u{j}": f(peer_u[l]), f"v{j}": f(peer_v[l]), f"convw{j}": convwT[l], f"lq1_{j}": f(lambda_q1[l]),
                f"lk1_{j}": f(lambda_k1[l]), f"lq2_{j}": f(lambda_q2[l]), f"lk2_{j}": f(lambda_k2[l]), f"subg{j}": f(subln_g[l]),
                f"ln1g{j}": f(ln1_g[l]), f"ln1b{j}": f(ln1_b[l]), f"ln2g{j}": f(ln2_g[l]), f"ln2b{j}": f(ln2_b[l])}

    iota16 = np.ascontiguousarray(np.broadcast_to(np.arange(16, dtype=np.float32)[None, :], (128, 16)))
    common = dict(lnin_g=f(ln_in_g), lnin_b=f(ln_in_b), rel_bias=f(rel_bias), identF=identF, identB=identB, zmask=zmask, dist=dist, iota16=iota16)
    hcur = [np.ascontiguousarray(h0[c * NSEQ:(c + 1) * NSEQ]) for c in range(ncore)]
    NLL = LAYERS_PER_LAUNCH
    for l0 in range(0, DEPTH, NLL):
        nc = _get_nc(NLL, l0 == 0)
        base = dict(common)
        base["laminit"] = lam_init[l0:l0 + NLL].copy()
        for j in range(NLL):
            base.update(layer_inputs(l0 + j, j))
        in_maps = []
        for c in range(ncore):
            m = dict(base)
            m["h_in"] = hcur[c]
            in_maps.append(m)
        res = run_bass_kernel_spmd(nc, in_maps, core_ids=list(range(ncore)))
        hcur = [np.asarray(r["h_out"]) for r in res.results]
    out = np.concatenate(hcur, axis=0)[:, NMETA:L, :]
    return np.ascontiguousarray(out.astype(np.float32))
```
